# Optimizing a Trainium2 kernel written in Bass

```python
import jax, jax.numpy as jnp
from jax import lax
import numpy as np

D_MODEL = 1024
BATCH = 32
SEQ = 256
DEPTH = 2
DEC_BATCH = 8
DEC_SEQ = 4096
PAST_LEN = 256

GRID_W = 64
N_MIXERS = 2
N_CONV_LAYERS = (DEPTH + 1) // 2
N_ATTN_LAYERS = DEPTH // 2
HEAD_DIM = 128
N_HEADS = D_MODEL // HEAD_DIM
N_KV_HEADS = 2
GROUP = N_HEADS // N_KV_HEADS
Q_DIM = N_HEADS * HEAD_DIM
KV_DIM = N_KV_HEADS * HEAD_DIM
QKV_DIM = Q_DIM + 2 * KV_DIM
AXIS_DIM = HEAD_DIM // 2
ROPE_THETA = 10000.0
Q_BLOCK = 128
ATTN_SCALE = HEAD_DIM ** -0.5
CONV_WIDTH = 31
FFN_DIM = 2816
FFN_CONV_WIDTH = 3
N_MOD = 6
EPS = 1e-6

kernel_name = "hybrid_conv_gqa_diffusion_step"


def rmsnorm(x, g):
    x32 = x.astype(jnp.float32)
    y = x32 * lax.rsqrt(jnp.mean(x32 * x32, axis=-1, keepdims=True) + EPS)
    return (y * g.astype(jnp.float32)).astype(x.dtype)


def layernorm(x, g, b):
    x32 = x.astype(jnp.float32)
    mu = jnp.mean(x32, axis=-1, keepdims=True)
    xc = x32 - mu
    y = xc * lax.rsqrt(jnp.mean(xc * xc, axis=-1, keepdims=True) + EPS)
    return (y * g.astype(jnp.float32) + b.astype(jnp.float32)).astype(x.dtype)


def modulate(h, shift, scale):
    return h * (1 + scale) + shift


def dwconv(x, w, b):
    C = x.shape[-1]
    y = lax.conv_general_dilated(x, w[:, None, :].astype(x.dtype), window_strides=(1,), padding='SAME',
                                 dimension_numbers=('NWC', 'WIO', 'NWC'), feature_group_count=C)
    return y + b


def axial_rope(x):
    L = x.shape[1]
    rows = L // GRID_W
    row = jnp.repeat(jnp.arange(rows), GRID_W)
    col = jnp.tile(jnp.arange(GRID_W), rows)
    freqs = ROPE_THETA ** (-jnp.arange(0, AXIS_DIM, 2, dtype=jnp.float32) / AXIS_DIM)

    def rot(xp, pos):
        ang = pos.astype(jnp.float32)[:, None] * freqs[None, :]
        cos = jnp.concatenate([jnp.cos(ang), jnp.cos(ang)], -1)[None, :, None, :]
        sin = jnp.concatenate([jnp.sin(ang), jnp.sin(ang)], -1)[None, :, None, :]
        x1, x2 = jnp.split(xp, 2, axis=-1)
        return xp * cos + jnp.concatenate([-x2, x1], -1) * sin

    x32 = x.astype(jnp.float32)
    out = jnp.concatenate([rot(x32[..., :AXIS_DIM], row), rot(x32[..., AXIS_DIM:], col)], -1)
    return out.astype(x.dtype)


def attend(q, k, v):
    B, S = q.shape[0], q.shape[1]
    nb = S // Q_BLOCK
    qb = q.reshape(B, nb, Q_BLOCK, N_KV_HEADS, GROUP, HEAD_DIM).transpose(1, 0, 2, 3, 4, 5)

    def block(qblk):
        s = jnp.einsum('bqkgd,btkd->bkgqt', qblk, k).astype(jnp.float32) * ATTN_SCALE
        p = jax.nn.softmax(s, axis=-1).astype(v.dtype)
        return jnp.einsum('bkgqt,btkd->bqkgd', p, v)

    o = lax.map(block, qb)
    return o.transpose(1, 0, 2, 3, 4, 5).reshape(B, S, Q_DIM)


def gqa_mixer(h, w_qkv, q_g, k_g, w_o, ctx_k, ctx_v):
    B, L, _ = h.shape
    qkv = h @ w_qkv
    q, k, v = jnp.split(qkv, [Q_DIM, Q_DIM + KV_DIM], axis=-1)
    q = rmsnorm(q.reshape(B, L, N_HEADS, HEAD_DIM), q_g)
    k = rmsnorm(k.reshape(B, L, N_KV_HEADS, HEAD_DIM), k_g)
    v = v.reshape(B, L, N_KV_HEADS, HEAD_DIM)
    if ctx_k is None:
        k_all, v_all = k, v
    else:
        q = axial_rope(q)
        k_all = jnp.concatenate([axial_rope(k), ctx_k], axis=1)
        v_all = jnp.concatenate([v, ctx_v], axis=1)
    return attend(q, k_all, v_all) @ w_o, k, v


def conformer_conv(h, w_pw1, dw_w, dw_b, ln_g, ln_b, w_pw2):
    a, g = jnp.split(h @ w_pw1, 2, axis=-1)
    u = a * jax.nn.sigmoid(g)
    u = dwconv(u, dw_w, dw_b)
    u = jax.nn.silu(layernorm(u, ln_g, ln_b))
    return u @ w_pw2


def conv_ffn(h, w_up, conv_w, conv_b, w_down):
    gate, val = jnp.split(h @ w_up, 2, axis=-1)
    gate = dwconv(gate, conv_w, conv_b)
    return (jax.nn.silu(gate) * val) @ w_down


def setup_inputs(seed: int = 0) -> dict:
    key = jax.random.key(seed)
    ks = jax.random.split(key, 32)
    f32 = jnp.float32

    def nrm(k, shape, scale):
        return jax.random.normal(k, shape, f32) * scale

    return {
        'x_prompt': nrm(ks[0], (BATCH, SEQ, D_MODEL), 1.0),
        'x_sample': nrm(ks[1], (DEC_BATCH, DEC_SEQ, D_MODEL), 1.0),
        'cache_attn_k': nrm(ks[2], (DEC_BATCH, N_ATTN_LAYERS, PAST_LEN, N_KV_HEADS, HEAD_DIM), 1.0),
        'cache_attn_v': nrm(ks[3], (DEC_BATCH, N_ATTN_LAYERS, PAST_LEN, N_KV_HEADS, HEAD_DIM), 1.0),
        'c': nrm(ks[4], (DEC_BATCH, D_MODEL), 1.0),
        'c_ctx': nrm(ks[5], (D_MODEL,), 1.0),
        'ada_w': nrm(ks[6], (DEPTH, D_MODEL, N_MOD * D_MODEL), 0.5 * D_MODEL ** -0.5),
        'ada_b': nrm(ks[7], (DEPTH, N_MOD * D_MODEL), 0.02),
        'norm1_g': 1.0 + nrm(ks[8], (DEPTH, D_MODEL), 0.05),
        'norm2_g': 1.0 + nrm(ks[9], (DEPTH, D_MODEL), 0.05),
        'conv_w_pw1': nrm(ks[10], (N_CONV_LAYERS, D_MODEL, 2 * D_MODEL), D_MODEL ** -0.5),
        'conv_dw_w': nrm(ks[11], (N_CONV_LAYERS, CONV_WIDTH, D_MODEL), CONV_WIDTH ** -0.5),
        'conv_dw_b': nrm(ks[12], (N_CONV_LAYERS, D_MODEL), 0.02),
        'conv_ln_g': 1.0 + nrm(ks[13], (N_CONV_LAYERS, D_MODEL), 0.05),
        'conv_ln_b': nrm(ks[14], (N_CONV_LAYERS, D_MODEL), 0.02),
        'conv_w_pw2': nrm(ks[15], (N_CONV_LAYERS, D_MODEL, D_MODEL), D_MODEL ** -0.5),
        'attn_w_qkv': nrm(ks[16], (N_ATTN_LAYERS, D_MODEL, QKV_DIM), D_MODEL ** -0.5),
        'attn_q_g': 1.0 + nrm(ks[17], (N_ATTN_LAYERS, HEAD_DIM), 0.05),
        'attn_k_g': 1.0 + nrm(ks[18], (N_ATTN_LAYERS, HEAD_DIM), 0.05),
        'attn_w_o': nrm(ks[19], (N_ATTN_LAYERS, Q_DIM, D_MODEL), Q_DIM ** -0.5),
        'ffn_w_up': nrm(ks[20], (DEPTH, D_MODEL, 2 * FFN_DIM), D_MODEL ** -0.5),
        'ffn_conv_w': nrm(ks[21], (DEPTH, FFN_CONV_WIDTH, FFN_DIM), FFN_CONV_WIDTH ** -0.5),
        'ffn_conv_b': nrm(ks[22], (DEPTH, FFN_DIM), 0.02),
        'ffn_w_down': nrm(ks[23], (DEPTH, FFN_DIM, D_MODEL), FFN_DIM ** -0.5),
        'final_g': 1.0 + nrm(ks[24], (D_MODEL,), 0.05),
    }


def reference(x_prompt, x_sample, cache_attn_k, cache_attn_v, c, c_ctx,
              ada_w, ada_b, norm1_g, norm2_g,
              conv_w_pw1, conv_dw_w, conv_dw_b, conv_ln_g, conv_ln_b, conv_w_pw2,
              attn_w_qkv, attn_q_g, attn_k_g, attn_w_o,
              ffn_w_up, ffn_conv_w, ffn_conv_b, ffn_w_down, final_g):

    def trunk(x, cond, ctx_k_all, ctx_v_all):
        ks_out, vs_out = [], []
        for i in range(DEPTH):
            mod = (jax.nn.silu(cond) @ ada_w[i] + ada_b[i])[:, None, :]
            sh1, sc1, g1, sh2, sc2, g2 = jnp.split(mod, N_MOD, axis=-1)
            h = modulate(rmsnorm(x, norm1_g[i]), sh1, sc1)
            j = i // N_MIXERS
            if i % N_MIXERS == 0:
                y = conformer_conv(h, conv_w_pw1[j], conv_dw_w[j], conv_dw_b[j],
                                   conv_ln_g[j], conv_ln_b[j], conv_w_pw2[j])
            else:
                ck = None if ctx_k_all is None else ctx_k_all[:, j]
                cv = None if ctx_v_all is None else ctx_v_all[:, j]
                y, k, v = gqa_mixer(h, attn_w_qkv[j], attn_q_g[j], attn_k_g[j], attn_w_o[j], ck, cv)
                ks_out.append(k)
                vs_out.append(v)
            x = x + g1 * y
            h = modulate(rmsnorm(x, norm2_g[i]), sh2, sc2)
            x = x + g2 * conv_ffn(h, ffn_w_up[i], ffn_conv_w[i], ffn_conv_b[i], ffn_w_down[i])
        return rmsnorm(x, final_g), ks_out, vs_out

    y_prompt, ctx_ks, ctx_vs = trunk(x_prompt, c_ctx[None, :], None, None)
    new_k = jnp.stack(ctx_ks, axis=1)
    new_v = jnp.stack(ctx_vs, axis=1)

    y_sample, _, _ = trunk(x_sample, c, cache_attn_k, cache_attn_v)

    return (y_prompt, y_sample, new_k, new_v)
```

```python
import contextlib
import numpy as np
import concourse.bass as bass
import concourse.mybir as mybir
from concourse.bass_utils import run_bass_kernel_spmd

F32 = mybir.dt.float32
BF16 = mybir.dt.bfloat16
AF = mybir.ActivationFunctionType
ALU = mybir.AluOpType

ENGINES = ("tensor", "vector", "scalar", "gpsimd", "sync")

D = 1024
NCH = 8
FFN = 2816
FCH = 22
HD = 128
EPS = 1e-6
ATTN_SCALE = HD ** -0.5
SEQ_S = 4096
SEQ_P = 256
PAST = 256
WBMAX = 576
SLOT_E = 4864
NSLOT = 4
TS = 456


class Op:
    __slots__ = ("eng", "fn", "deps", "token_sem", "token_val", "signal", "is_dma", "dma_sem")


class Prog:
    def __init__(self, nc):
        self.nc = nc
        self.ops = []
        self.per_eng = {e: [] for e in ENGINES}
        self.last_writer = {}
        self.readers = {}

    def op(self, eng, fn, reads=(), writes=(), dma=None):
        o = Op()
        o.eng = eng
        o.fn = fn
        o.is_dma = dma is not None
        o.dma_sem = dma
        o.signal = o.is_dma
        deps = set()
        for k in reads:
            w = self.last_writer.get(k)
            if w is not None:
                deps.add(w)
        for k in writes:
            w = self.last_writer.get(k)
            if w is not None:
                deps.add(w)
            for r in self.readers.get(k, ()):
                deps.add(r)
        keep = []
        rset = set(reads)
        for d in deps:
            if d.eng == eng and not d.is_dma and not o.is_dma:
                if eng == "tensor":
                    continue
                raw = any(self.last_writer.get(k) is d for k in rset)
                if not raw:
                    continue
            keep.append(d)
        o.deps = keep
        for d in keep:
            d.signal = True
        for k in reads:
            self.readers.setdefault(k, []).append(o)
        for k in writes:
            self.last_writer[k] = o
            self.readers[k] = []
        self.ops.append(o)
        self.per_eng[eng].append(o)
        return o

    def emit(self, sem_ctx):
        nc = self.nc
        for e in ENGINES:
            cnt = 0
            for o in self.per_eng[e]:
                if not o.is_dma and o.signal:
                    cnt += 1
                    o.token_sem = sem_ctx["eng_" + e]
                    o.token_val = cnt
        dc = {}
        for o in self.ops:
            if o.is_dma:
                dc[o.dma_sem] = dc.get(o.dma_sem, 0) + 16
                o.token_sem = sem_ctx[o.dma_sem]
                o.token_val = dc[o.dma_sem]

        def run(e, eng):
            known = {}
            for o in self.per_eng[e]:
                waits = {}
                for d in o.deps:
                    key = id(d.token_sem)
                    if known.get(key, 0) >= d.token_val:
                        continue
                    if key not in waits or waits[key][1] < d.token_val:
                        waits[key] = (d.token_sem, d.token_val)
                for key, (s, v) in waits.items():
                    eng.wait_ge(s, v)
                    known[key] = v
                ins = o.fn(eng)
                if o.is_dma:
                    ins.then_inc(o.token_sem, 16)
                elif o.signal:
                    ins.then_inc(o.token_sem, 1)
            if e == "sync":
                for name, v in dc.items():
                    eng.wait_ge(sem_ctx[name], v)

        with nc.Block() as block:
            @block.tensor
            def _(eng):
                run("tensor", eng)

            @block.vector
            def _(eng):
                run("vector", eng)

            @block.scalar
            def _(eng):
                run("scalar", eng)

            @block.gpsimd
            def _(eng):
                run("gpsimd", eng)

            @block.sync
            def _(eng):
                run("sync", eng)


def _perm128():
    p = np.arange(128)
    q = p % 64
    return (p // 64) * 64 + np.where(q < 32, q + 32, q - 32)


PERM = _perm128()


def _kc(W, cols):
    K = W.shape[0]
    a = W[:, cols].reshape(K // 128, 128, len(cols))
    return np.ascontiguousarray(a.transpose(1, 0, 2)).reshape(128, -1)


def _diag(wtaps, ch):
    nt = wtaps.shape[0]
    out = np.zeros((128, nt, 128), np.float32)
    idx = np.arange(128)
    out[idx, :, idx] = wtaps[:, ch * 128:(ch + 1) * 128].T
    return out.reshape(128, -1)


def _r(a, b):
    return np.arange(a, b)


def unit_catalogue():
    cat = []

    def ada_units(l):
        for j in range(24):
            cat.append((("ada", l, j), 2048,
                        lambda I, l=l, j=j: _kc(I["ada_w"][l], _r(j * 256, j * 256 + 256))))
    ada_units(0)

    def phaseA_units():
        for u in range(4):
            def rec(I, u=u):
                cols = np.concatenate([_r(2 * u * 128, (2 * u + 2) * 128),
                                       _r(1024 + 2 * u * 128, 1024 + (2 * u + 2) * 128)])
                return _kc(I["conv_w_pw1"][0], cols)
            cat.append((("pw1", u), 4096, rec))
            for ch in (2 * u, 2 * u + 1):
                cat.append((("dg31", ch), 31 * 128, lambda I, ch=ch: _diag(I["conv_dw_w"][0], ch)))
        for u in range(2):
            cat.append((("pw2", u), 4096, lambda I, u=u: _kc(I["conv_w_pw2"][0], _r(u * 512, u * 512 + 512))))
        ffn_units(0)
        ada_units(1)
        def reck(I):
            cols = np.concatenate([_r(1024, 1280), 1024 + PERM, 1152 + PERM])
            return _kc(I["attn_w_qkv"][0], cols)
        cat.append((("kpk",), 4096, reck))
        cat.append((("v",), 2048, lambda I: _kc(I["attn_w_qkv"][0], _r(1280, 1536))))

    def ffn_units(l):
        for u in range(11):
            def rec(I, u=u, l=l):
                cols = np.concatenate([_r(2 * u * 128, (2 * u + 2) * 128),
                                       _r(FFN + 2 * u * 128, FFN + (2 * u + 2) * 128)])
                a = _kc(I["ffn_w_up"][l], cols)
                d0 = _diag(I["ffn_conv_w"][l], 2 * u)
                d1 = _diag(I["ffn_conv_w"][l], 2 * u + 1)
                return np.concatenate([a, d0, d1], axis=1)
            cat.append((("up", l, u), 4864, rec))
        for oc in range(8):
            cat.append((("down", l, oc), FCH * 128,
                        lambda I, oc=oc, l=l: _kc(I["ffn_w_down"][l], _r(oc * 128, oc * 128 + 128))))

    def phaseB_units():
        for u in range(4):
            def rec(I, u=u):
                h0, h1 = 2 * u, 2 * u + 1
                cols = np.concatenate([_r(h0 * 128, h0 * 128 + 256), h0 * 128 + PERM, h1 * 128 + PERM])
                return _kc(I["attn_w_qkv"][0], cols)
            cat.append((("q", u), 4096, rec))
        for u in range(2):
            cat.append((("wo", u), 4096, lambda I, u=u: _kc(I["attn_w_o"][0], _r(u * 512, u * 512 + 512))))
        ffn_units(1)

    phaseA_units()
    phaseB_units()
    offs = {}
    off = 0
    order = []
    for name, E, rec in cat:
        offs[name] = (off, E, rec)
        order.append(name)
        off += 128 * E
    return order, offs, off


def const_map():
    m = {}
    c = 0

    def add(name, n):
        nonlocal c
        m[name] = (c, n)
        c += n
    add("adab", 2 * 96)
    add("n1g", 16)
    add("n2g", 16)
    add("fg", 8)
    add("dwb", 8)
    add("lng", 8)
    add("lnb", 8)
    add("fcb", 2 * FCH)
    add("qg", 1)
    add("pqg", 1)
    add("kg", 1)
    add("pkg", 1)
    add("eps", 1)
    return m, c


CMAP, NCONST = const_map()


class Tile:
    pass


def make_tiles():
    tiles = []
    for pt in range(2):
        t = Tile()
        t.kind = "p"
        t.S = 2
        t.L = 256
        t.WB = 288
        t.aE, t.bE = 16, 272
        t.aM, t.bM = 16, 272
        t.s = 0
        t.pt = pt
        t.cond = 1
        tiles.append(t)
    s = 0
    while s < SEQ_S:
        e = min(s + TS, SEQ_S)
        t = Tile()
        t.kind = "s"
        t.S = 1
        t.L = e - s
        t.WB = t.L + 32
        lo = max(s - 16, 0)
        hi = min(e + 16, SEQ_S)
        t.aE, t.bE = lo - (s - 16), hi - (s - 16)
        lo = max(s - 1, 0)
        hi = min(e + 1, SEQ_S)
        t.aM, t.bM = lo - (s - 16), hi - (s - 16)
        t.s = s
        t.cond = 0
        tiles.append(t)
        s = e
    for t in tiles:
        t.aO, t.bO = 16, 16 + t.L
    return tiles


DEBUG = False
USE_LN = False


def build_program():
    nc = bass.Bass("TRN2", target_bir_lowering=False)
    order, offs, wtotal = unit_catalogue()

    def din(name, shape, dt=F32):
        return nc.dram_tensor(name, shape, dt, kind="ExternalInput").ap()

    def dout(name, shape, dt=F32):
        return nc.dram_tensor(name, shape, dt, kind="ExternalOutput").ap()

    xs_d = din("xs", [128, NCH, SEQ_S])
    xp_d = din("xp", [128, NCH, 4 * SEQ_P])
    cond_d = din("cond", [128, 16])
    ckT_d = din("ckT", [128, 2 * PAST])
    cv_d = din("cv", [128, 2 * 256])
    cst_d = din("cst", [128, NCONST])
    cos_d = din("cosT", [128, SEQ_S + SEQ_P])
    sin_d = din("sinT", [128, SEQ_S + SEQ_P])
    wf_d = din("wf", [wtotal])
    ys_d = dout("ys", [128, NCH, SEQ_S])
    yp_d = dout("yp", [128, NCH, 4 * SEQ_P])
    nk_d = dout("nk", [128, 2, 4 * SEQ_P])
    nv_d = dout("nv", [4 * SEQ_P, 256])
    if DEBUG:
        dbg_c = dout("dbg_c", [128, NCH * WBMAX], BF16)
        dbg_x = dout("dbg_x", [128, NCH * WBMAX])
        dbg_q = dout("dbg_q", [128, NCH * WBMAX], BF16)
        dbg_k = dout("dbg_k", [128, 2 * (SEQ_S + PAST)], BF16)
        dbg_v = dout("dbg_v", [128, 40 * 256], BF16)
    wb_d = nc.dram_tensor("wb", [wtotal], BF16, kind="Internal").ap()
    x1_d = nc.dram_tensor("x1s", [128, NCH, SEQ_S], F32, kind="Internal").ap()
    h1_d = nc.dram_tensor("h1s", [128, NCH, SEQ_S], BF16, kind="Internal").ap()

    sem_names = ["eng_" + e for e in ENGINES] + ["d_const", "d_rope", "d_nk", "d_h1", "d_hl"] + ["d_nv%d" % i for i in range(4)] + ["d_x%d" % i for i in range(NCH)] + ["d_s%d" % i for i in range(NCH)] + ["d_y%d" % i for i in range(4)] + \
                ["d_cvt%d" % i for i in range(8)] + ["d_w%d" % i for i in range(NSLOT)]

    with contextlib.ExitStack() as st:
        sems = {n: st.enter_context(nc.semaphore(n)) for n in sem_names}

        def sb(name, cols, dt):
            return st.enter_context(nc.sbuf_tensor("s_" + name, [128, cols], dt))

        xbuf = sb("xbuf", NCH * WBMAX, F32)
        hbuf = sb("hbuf", NCH * WBMAX, BF16)
        ubuf = sb("ubuf", NCH * WBMAX, BF16)
        cbuf = sb("cbuf", NCH * WBMAX, BF16)
        actb = sb("actb", FCH * 512, BF16)
        KT = sb("KT", 2 * (SEQ_S + PAST + 512), BF16)
        NVCH = 42
        Vb = sb("Vb", NVCH * 256, BF16)
        slots = [sb("wslot%d" % i, SLOT_E, BF16) for i in range(NSLOT)]
        NSQ = 4
        sqt = [sb("sqt%d" % i, WBMAX, BF16) for i in range(NSQ)]
        NTF = 4
        tft = [sb("tft%d" % i, WBMAX, F32) for i in range(NTF)]
        NSG = 3
        sgt = [sb("sgt%d" % i, 1152, BF16) for i in range(NSG)]
        NGT = 4
        gtt = [sb("gtt%d" % i, WBMAX, BF16) for i in range(NGT)]
        slt = [sb("slt%d" % i, 512, BF16) for i in range(NGT)]
        st_mean = sb("st_mean", WBMAX, F32)
        st_rstd = sb("st_rstd", WBMAX, F32)
        st_nmr = sb("st_nmr", WBMAX, F32)
        st_t = sb("st_t", WBMAX, F32)
        st_rq = sb("st_rq", WBMAX, F32)
        cosb = sb("cosb", 512, F32)
        sinb = sb("sinb", 512, F32)
        knew = sb("knew", 2 * 512, F32)
        vnews = [sb("vnew%d" % i, 256, F32) for i in range(4)]
        cst = sb("cst", NCONST, F32)
        condb = sb("condb", 16, F32)
        scondf = sb("scondf", 16, F32)
        modb = sb("modb", 2 * 96, F32)
        der = sb("der", 2 * 2 * 6 * 8, F32)
        ckf = sb("ckf", 512, F32)
        cvf = sb("cvf", 512, F32)
        dummy = sb("dummy", 8, F32)
        ones_m = sb("ones_m", 128, BF16)
        ones_h = sb("ones_h", 128, BF16)
        ones_1 = sb("ones_1", 128, BF16)
        PS = [st.enter_context(nc.psum_tensor("ps%d" % i, [128, 1024], F32)) for i in range(4)]

        P = Prog(nc)

        def bank_ap(b):
            return PS[b // 2][:, (b % 2) * 512:(b % 2) * 512 + 512]

        def psv(b, S, w, n=128):
            return PS[b // 2][0:n, (b % 2) * 512:(b % 2) * 512 + S * w].rearrange("p (s w) -> p s w", s=S)

        def cols(buf, ch, T, a, b, stride=WBMAX):
            base = ch * stride
            return buf[:, base:base + T.S * T.WB].rearrange("p (s w) -> p s w", s=T.S)[:, :, a:b]

        def tv(tbuf, S, w):
            return tbuf[:, 0:S * w].rearrange("p (s w) -> p s w", s=S)

        def cc(name, i=0, n=1):
            c0, _ = CMAP[name]
            return cst[:, c0 + i:c0 + i + n]

        bank_ctr = [0]
        att_ctr = [0]

        def next_bank(avoid=()):
            assert len(set(avoid)) < 8
            while True:
                b = bank_ctr[0] % 8
                bank_ctr[0] += 1
                if b not in avoid:
                    return b

        rot = {"sq": 0, "tf": 0, "sg": 0, "gt": 0}

        def nxt(k, n):
            v = rot[k] % n
            rot[k] += 1
            return v

        wctr = [0]

        def wload(name):
            off, E, _ = offs[name]
            s = wctr[0] % NSLOT
            wctr[0] += 1
            src = wb_d[off:off + 128 * E].rearrange("(p e) -> p e", p=128)
            P.op("sync", lambda e: e.dma_start(out=slots[s][:, 0:E], in_=src),
                 reads=[("wb", name)], writes=[("ws", s)], dma="d_w%d" % s)
            return slots[s], ("ws", s)

        def wload_f32(name):
            off, E, _ = offs[name]
            s = wctr[0] % NSLOT
            wctr[0] += 1
            src = wf_d[off:off + 128 * E].rearrange("(p e) -> p e", p=128)
            dstf = slots[s][:, 0:2 * E].bitcast(F32)
            P.op("sync", lambda e: e.dma_start(out=dstf, in_=src), reads=[], writes=[("ws", s)], dma="d_w%d" % s)
            return dstf, ("ws", s)

        def wblk(slot, kc, j, ncols=512, w=128):
            base = kc * ncols + j * 128
            return slot[:, base:base + w]

        for name in order:
            off, E, _ = offs[name]
            if name[0] == "ada":
                continue
            P.op("gpsimd", lambda e, off=off, E=E: e.dma_start(out=wb_d[off:off + 128 * E].rearrange("(p e) -> p e", p=128),
                                                               in_=wf_d[off:off + 128 * E].rearrange("(p e) -> p e", p=128)),
                 reads=[], writes=[("wb", name)], dma="d_cvt%d" % (order.index(name) % 8))
        P.op("sync", lambda e: e.dma_start(out=cst[:, :], in_=cst_d), writes=["cst"], dma="d_const")
        P.op("sync", lambda e: e.dma_start(out=condb[:, :], in_=cond_d), writes=["cst"], dma="d_const")
        P.op("sync", lambda e: e.dma_start(out=ckf[:, :], in_=ckT_d), writes=["cst"], dma="d_const")
        P.op("sync", lambda e: e.dma_start(out=cvf[:, :], in_=cv_d), writes=["cst"], dma="d_const")
        P.op("vector", lambda e: e.memset(ones_m[:, :], 1.0 / 1024), writes=["ones"])
        P.op("vector", lambda e: e.memset(ones_h[:, :], 1.0 / 128), writes=["ones"])
        P.op("vector", lambda e: e.memset(ones_1[:, :], 1.0), writes=["ones"])
        P.op("vector", lambda e: e.memset(dummy[:, :], 1.0), writes=["dummy_in"])
        P.op("scalar", lambda e: e.activation(out=scondf[:, :], in_=condb[:, :], func=AF.Silu),
             reads=["cst"], writes=["scond"])

        def emit_mods(l):
            b = next_bank()
            for j in range(24):
                slotf, wk = wload_f32(("ada", l, j))
                for jj in range(2):
                    m = j * 2 + jj

                    def mm(e, slotf=slotf, jj=jj, m=m, b=b):
                        ins = None
                        for kc in range(8):
                            ins = e.matmul(bank_ap(b)[:, 2 * m:2 * m + 2], slotf[:, kc * 256 + jj * 128:kc * 256 + (jj + 1) * 128],
                                           scondf[:, 2 * kc:2 * kc + 2], start=(kc == 0), stop=(kc == 7))
                        return ins
                    P.op("tensor", mm, reads=[wk, "scond"], writes=[("ps", b)])
            P.op("vector", lambda e, l=l, b=b: e.tensor_tensor(out=modb[:, l * 96:(l + 1) * 96], in0=bank_ap(b)[:, 0:96],
                                                               in1=cc("adab", l * 96, 96), op=ALU.add),
                 reads=[("ps", b), "cst"], writes=[("modb", l)])
            emit_der(l)

        def modv(l, which, cond):
            base = l * 96 + which * 16
            return modb[:, base:base + 16].rearrange("p (c t) -> p c t", t=2)[:, :, cond:cond + 1]

        def derv(l, cond, k, ch=None):
            base = ((l * 2 + cond) * 6 + k) * 8
            if ch is None:
                return der[:, base:base + 8]
            return der[:, base + ch:base + ch + 1]

        def emit_der(l):
            for cond in range(2):
                for k, (which, gname) in enumerate([(1, "n1g"), (0, None), (2, None), (4, "n2g"), (3, None), (5, None)]):
                    dst = derv(l, cond, k).rearrange("p (c t) -> p c t", t=1)
                    src = modv(l, which, cond)
                    if gname is not None:
                        g = cc(gname, l * 8, 8).rearrange("p (c t) -> p c t", t=1)
                        P.op("vector", lambda e, dst=dst, src=src, g=g: e.scalar_tensor_tensor(
                            out=dst, in0=src, scalar=1.0, in1=g, op0=ALU.add, op1=ALU.mult),
                            reads=[("modb", l), "cst"], writes=[("der", l)])
                    else:
                        P.op("vector", lambda e, dst=dst, src=src: e.tensor_copy(out=dst, in_=src),
                             reads=[("modb", l)], writes=[("der", l)])

        KTW = SEQ_S + PAST + 512
        KTP = SEQ_S + PAST
        VIP = 38

        def kt_cols(g, a, b):
            return KT[:, g * KTW + a:g * KTW + b]

        def emit_cache():
            for g in range(2):
                P.op("vector", lambda e, g=g: e.tensor_copy(out=kt_cols(g, 0, PAST), in_=ckf[:, g * PAST:(g + 1) * PAST]),
                     reads=["cst"], writes=[("KT", g, "c"), ("KTall", g)])
            for j in range(2):
                P.op("vector", lambda e, j=j: e.tensor_copy(out=Vb[:, j * 256:(j + 1) * 256], in_=cvf[:, j * 256:(j + 1) * 256]),
                     reads=["cst"], writes=[("V", j)])

        def xkeys():
            return [("x", ch) for ch in range(NCH)]

        def hkeys():
            return [("h", ch) for ch in range(NCH)]

        def xview(ch, c0, c1):
            return xbuf[:, ch * WBMAX + c0:ch * WBMAX + c1]

        def load_x(T):
            for ch in range(NCH):
                if T.kind == "p":
                    for seg in range(2):
                        q = 2 * T.pt + seg
                        dst = xview(ch, seg * T.WB + 16, seg * T.WB + 16 + 256)
                        src = xp_d[:, ch, q * 256:(q + 1) * 256]
                        P.op("sync", lambda e, dst=dst, src=src: e.dma_start(out=dst, in_=src), writes=[("x", ch)], dma="d_x%d" % ch)
                else:
                    w = T.bE - T.aE
                    t0 = T.s - 16 + T.aE
                    dst = xview(ch, T.aE, T.bE)
                    src = xs_d[:, ch, t0:t0 + w]
                    P.op("sync", lambda e, dst=dst, src=src: e.dma_start(out=dst, in_=src), writes=[("x", ch)], dma="d_x%d" % ch)

        def key_of(buf):
            return id(buf)

        def rstd_stage(b, S, w, dst):
            if USE_LN:
                P.op("scalar", lambda e: e.activation(out=tv(dst, S, w), in_=psv(b, S, w), func=AF.Ln,
                                                      bias=cc("eps"), scale=1.0),
                     reads=[("ps", b), "cst"], writes=[key_of(dst)])
                P.op("scalar", lambda e: e.activation(out=tv(dst, S, w), in_=tv(dst, S, w), func=AF.Exp, scale=-0.5),
                     reads=[key_of(dst)], writes=[key_of(dst)])
            else:
                P.op("scalar", lambda e: e.activation(out=tv(dst, S, w), in_=psv(b, S, w), func=AF.Sqrt,
                                                      bias=cc("eps"), scale=1.0),
                     reads=[("ps", b), "cst"], writes=[key_of(dst)])
                P.op("vector", lambda e: e.reciprocal(out=tv(dst, S, w), in_=tv(dst, S, w)),
                     reads=[key_of(dst)], writes=[key_of(dst)])

        def square_op(idx, out_ap, in_ap, rkeys, wkeys):
            m = idx % 8
            if m in (0, 3, 6):
                P.op("gpsimd", lambda e: e.tensor_tensor(out=out_ap, in0=in_ap, in1=in_ap, op=ALU.mult), reads=rkeys, writes=wkeys)
            else:
                P.op("scalar", lambda e: e.activation(out=out_ap, in_=in_ap, func=AF.Square), reads=rkeys, writes=wkeys)

        def rms_mod(T, a, b, l, cond, kA, kB):
            S = T.S
            w = b - a
            bk = next_bank()
            act_warm(AF.Sqrt)
            for ch in range(NCH):
                i = nxt("sq", NSQ)
                square_op(ch, tv(sqt[i], S, w), cols(xbuf, ch, T, a, b), [("x", ch)], [("sq", i)])
                P.op("tensor", lambda e, ch=ch, i=i: e.matmul(psv(bk, S, w), ones_m[:, :], tv(sqt[i], S, w),
                                                              start=(ch == 0), stop=(ch == NCH - 1)),
                     reads=[("sq", i), "ones"], writes=[("ps", bk)])
            rstd_stage(bk, S, w, st_rstd)
            for ch in range(NCH):
                i = nxt("tf", NTF)
                P.op("vector", lambda e, ch=ch, i=i: e.scalar_tensor_tensor(
                    out=tv(tft[i], S, w), in0=cols(xbuf, ch, T, a, b), scalar=derv(l, cond, kA, ch),
                    in1=tv(st_rstd, S, w), op0=ALU.mult, op1=ALU.mult),
                    reads=[("x", ch), key_of(st_rstd), ("der", l)], writes=[("tf", i)])
                P.op("scalar", lambda e, ch=ch, i=i: e.activation(
                    out=cols(hbuf, ch, T, a, b), in_=tv(tft[i], S, w), func=AF.Identity,
                    bias=derv(l, cond, kB, ch), scale=1.0),
                    reads=[("tf", i), ("der", l)], writes=[("h", ch)])

        def resid_add(T, bk, oc, a, b, l, cond, kG):
            S = T.S
            w = b - a
            P.op("vector", lambda e: e.scalar_tensor_tensor(
                out=cols(xbuf, oc, T, a, b), in0=psv(bk, S, w), scalar=derv(l, cond, kG, oc),
                in1=cols(xbuf, oc, T, a, b), op0=ALU.mult, op1=ALU.add),
                reads=[("ps", bk), ("x", oc), ("der", l)], writes=[("x", oc)])

        def mm_dense(T, bk, slot, wk, ncols, j, src_buf, src_key, a, b, nk=NCH, w128=128, src_stride=WBMAX, src_cols=None):
            S = T.S
            w = b - a

            def mm(e):
                ins = None
                for kc in range(nk):
                    rhs = cols(src_buf, kc, T, a, b, stride=src_stride) if src_cols is None else src_cols(kc)
                    ins = e.matmul(psv(bk, S, w), wblk(slot, kc, j, ncols), rhs, start=(kc == 0), stop=(kc == nk - 1))
                return ins
            P.op("tensor", mm, reads=[wk] + [(src_key, kc) for kc in range(nk)], writes=[("ps", bk)])

        def mm_wave(T, outs, src_buf, src_key, nk=NCH):
            S = T.S
            wks = list({o[2] for o in outs})
            for kc in range(nk):
                def mm(e, kc=kc):
                    ins = None
                    for (bk, slot, wk, ncols, j, a, b) in outs:
                        ins = e.matmul(psv(bk, S, b - a), wblk(slot, kc, j, ncols), cols(src_buf, kc, T, a, b),
                                       start=(kc == 0), stop=(kc == nk - 1))
                    return ins
                P.op("tensor", mm, reads=wks + [(src_key, kc)], writes=[("ps", o[0]) for o in outs])

        def act_warm(func):
            P.op("scalar", lambda e: e.activation(out=dummy[:, 2:3], in_=dummy[:, 4:5], func=func), reads=["dummy_in"], writes=["dummy_out"])

        def conv_mixer(T, l, cond):
            S = T.S
            aE, bE, aM, bM = T.aE, T.bE, T.aM, T.bM
            wE = bE - aE
            wM = bM - aM
            rms_mod(T, aE, bE, l, cond, 0, 1)
            if aE > 0:
                for ch in range(NCH):
                    P.op("gpsimd", lambda e, ch=ch: e.memset(cols(ubuf, ch, T, 0, aE), 0.0), writes=[("u", ch)])
            if bE < T.WB:
                for ch in range(NCH):
                    P.op("gpsimd", lambda e, ch=ch: e.memset(cols(ubuf, ch, T, bE, T.WB), 0.0), writes=[("u", ch)])
            conv_banks = {}

            def do_conv(ch):
                dslot, dk = wload(("dg31", ch))
                bk = next_bank()
                conv_banks[ch] = bk

                def mm(e):
                    ins = None
                    for k in range(31):
                        ins = e.matmul(psv(bk, S, wM), dslot[:, k * 128:(k + 1) * 128],
                                       cols(ubuf, ch, T, aM + k - 15, bM + k - 15), start=(k == 0), stop=(k == 30))
                    return ins
                P.op("tensor", mm, reads=[dk, ("u", ch)], writes=[("ps", bk)])
                P.op("scalar", lambda e: e.activation(out=cols(cbuf, ch, T, aM, bM), in_=psv(bk, S, wM), func=AF.Identity,
                                                      bias=cc("dwb", ch), scale=0.5),
                     reads=[("ps", bk), "cst"], writes=[("c", ch)])

            pending = []

            def glu(ch, ba, bg):
                i = nxt("sg", NSG)
                P.op("scalar", lambda e: e.activation(out=tv(sgt[i], S, wE), in_=psv(bg, S, wE), func=AF.Tanh, scale=0.5),
                     reads=[("ps", bg)], writes=[("sg", i)])
                P.op("vector", lambda e: e.scalar_tensor_tensor(
                    out=cols(ubuf, ch, T, aE, bE), in0=tv(sgt[i], S, wE), scalar=1.0, in1=psv(ba, S, wE),
                    op0=ALU.add, op1=ALU.mult),
                    reads=[("ps", ba), ("sg", i)], writes=[("u", ch)])

            sl0, wk0 = wload(("pw1", 0))
            wb_ = [next_bank() for _ in range(4)]
            outs = []
            pairs = []
            for ui, (sl_, wk_) in enumerate(((sl0, wk0),)):
                for jj in range(2):
                    ba = wb_[ui * 4 + jj * 2]
                    bg = wb_[ui * 4 + jj * 2 + 1]
                    outs.append((ba, sl_, wk_, 512, jj, aE, bE))
                    outs.append((bg, sl_, wk_, 512, 2 + jj, aE, bE))
                    pairs.append((2 * ui + jj, ba, bg))
            mm_wave(T, outs, hbuf, "h")
            act_warm(AF.Tanh)
            for ch, ba, bg in pairs:
                glu(ch, ba, bg)
                pending.append(ch)
            while len(pending) > 2:
                do_conv(pending.pop(0))
            for u in range(1, 4):
                slot, wk = wload(("pw1", u))
                for jj in range(2):
                    ch = 2 * u + jj
                    ba = next_bank()
                    bg = next_bank()
                    mm_dense(T, ba, slot, wk, 512, jj, hbuf, "h", aE, bE)
                    mm_dense(T, bg, slot, wk, 512, 2 + jj, hbuf, "h", aE, bE)
                    glu(ch, ba, bg)
                for jj in range(2):
                    pending.append(2 * u + jj)
                while len(pending) > 2:
                    do_conv(pending.pop(0))
            while pending:
                do_conv(pending.pop(0))
            bm = next_bank()
            bq = next_bank()
            act_warm(AF.Sqrt)
            for ch in range(NCH):
                i = nxt("sq", NSQ)
                square_op(ch, tv(sqt[i], S, wM), cols(cbuf, ch, T, aM, bM), [("c", ch)], [("sq", i)])
                P.op("tensor", lambda e, ch=ch: e.matmul(psv(bm, S, wM), ones_m[:, :], cols(cbuf, ch, T, aM, bM),
                                                         start=(ch == 0), stop=(ch == NCH - 1)),
                     reads=[("c", ch), "ones"], writes=[("ps", bm)])
                P.op("tensor", lambda e, ch=ch, i=i: e.matmul(psv(bq, S, wM), ones_m[:, :], tv(sqt[i], S, wM),
                                                              start=(ch == 0), stop=(ch == NCH - 1)),
                     reads=[("sq", i), "ones"], writes=[("ps", bq)])
            km, kr, kn, kt = key_of(st_mean), key_of(st_rstd), key_of(st_nmr), key_of(st_t)
            P.op("scalar", lambda e: e.activation(out=tv(st_mean, S, wM), in_=psv(bm, S, wM), func=AF.Identity),
                 reads=[("ps", bm)], writes=[km])
            P.op("vector", lambda e: e.tensor_tensor(out=tv(st_t, S, wM), in0=tv(st_mean, S, wM), in1=tv(st_mean, S, wM), op=ALU.mult),
                 reads=[km], writes=[kt])
            P.op("vector", lambda e: e.tensor_tensor(out=tv(st_t, S, wM), in0=psv(bq, S, wM), in1=tv(st_t, S, wM), op=ALU.subtract),
                 reads=[("ps", bq), kt], writes=[kt])
            if USE_LN:
                P.op("scalar", lambda e: e.activation(out=tv(st_rstd, S, wM), in_=tv(st_t, S, wM), func=AF.Ln, bias=cc("eps"), scale=1.0),
                     reads=[kt, "cst"], writes=[kr])
                P.op("scalar", lambda e: e.activation(out=tv(st_rstd, S, wM), in_=tv(st_rstd, S, wM), func=AF.Exp, scale=-0.5),
                     reads=[kr], writes=[kr])
            else:
                P.op("scalar", lambda e: e.activation(out=tv(st_rstd, S, wM), in_=tv(st_t, S, wM), func=AF.Sqrt, bias=cc("eps"), scale=1.0),
                     reads=[kt, "cst"], writes=[kr])
                act_warm(AF.Silu)
                P.op("vector", lambda e: e.reciprocal(out=tv(st_rstd, S, wM), in_=tv(st_rstd, S, wM)), reads=[kr], writes=[kr])
            P.op("vector", lambda e: e.scalar_tensor_tensor(out=tv(st_nmr, S, wM), in0=tv(st_mean, S, wM), scalar=-1.0,
                                                            in1=tv(st_rstd, S, wM), op0=ALU.mult, op1=ALU.mult),
                 reads=[km, kr], writes=[kn])
            for ch in range(NCH):
                i = nxt("tf", NTF)
                P.op("vector", lambda e, ch=ch, i=i: e.tensor_tensor(out=tv(tft[i], S, wM), in0=cols(cbuf, ch, T, aM, bM),
                                                                     in1=tv(st_rstd, S, wM), op=ALU.mult),
                     reads=[("c", ch), kr], writes=[("tf", i)])
                P.op("gpsimd" if ch % 3 == 1 else "vector", lambda e, i=i: e.tensor_tensor(out=tv(tft[i], S, wM), in0=tv(tft[i], S, wM),
                                                              in1=tv(st_nmr, S, wM), op=ALU.add),
                     reads=[("tf", i), kn], writes=[("tf", i)])
                P.op("scalar", lambda e, ch=ch, i=i: e.activation(out=cols(hbuf, ch, T, aM, bM), in_=tv(tft[i], S, wM),
                                                                  func=AF.Silu, bias=cc("lnb", ch), scale=cc("lng", ch)),
                     reads=[("tf", i), "cst"], writes=[("h", ch)])
            sl0, wk0 = wload(("pw2", 0))
            sl1, wk1 = wload(("pw2", 1))
            wb_ = [next_bank() for _ in range(8)]
            outs = []
            for oc in range(8):
                sl_, wk_ = (sl0, wk0) if oc < 4 else (sl1, wk1)
                outs.append((wb_[oc], sl_, wk_, 512, oc % 4, aM, bM))
            mm_wave(T, outs[0:4], hbuf, "h")
            for oc in range(4):
                resid_add(T, wb_[oc], oc, aM, bM, l, cond, 2)
            mm_wave(T, outs[4:8], hbuf, "h")
            for oc in range(4, 8):
                resid_add(T, wb_[oc], oc, aM, bM, l, cond, 2)

        def ffn(T, l, cond, aG, bG, after_up=None):
            S = T.S
            aO, bO = T.aO, T.bO
            L = T.L
            wG = bG - aG
            rms_mod(T, aG, bG, l, cond, 3, 4)
            pend = []

            def post(item):
                ch, gi, bv, dslot, dk, dbase = item
                bc = next_bank()

                def mm(e):
                    ins = None
                    for k in range(3):
                        ins = e.matmul(psv(bc, S, L), dslot[:, dbase + k * 128:dbase + (k + 1) * 128],
                                       tv(gtt[gi], S, T.WB)[:, :, aO + k - 1:bO + k - 1], start=(k == 0), stop=(k == 2))
                    return ins
                P.op("tensor", mm, reads=[dk, ("gt", gi)], writes=[("ps", bc)])
                P.op("scalar", lambda e: e.activation(out=tv(slt[gi], S, L), in_=psv(bc, S, L), func=AF.Silu,
                                                      bias=cc("fcb", l * FCH + ch), scale=1.0),
                     reads=[("ps", bc), "cst"], writes=[("sl", gi)])
                P.op("vector", lambda e: e.tensor_tensor(out=actb[:, ch * 512:ch * 512 + S * L].rearrange("p (s w) -> p s w", s=S),
                                                         in0=psv(bv, S, L), in1=tv(slt[gi], S, L), op=ALU.mult),
                     reads=[("ps", bv), ("sl", gi)], writes=[("act", ch)])

            def gate_evac(gi, bg):
                if aG > aO - 1:
                    P.op("gpsimd", lambda e: e.memset(tv(gtt[gi], S, T.WB)[:, :, aO - 1:aG], 0.0), writes=[("gt", gi)])
                if bG < bO + 1:
                    P.op("gpsimd", lambda e: e.memset(tv(gtt[gi], S, T.WB)[:, :, bG:bO + 1], 0.0), writes=[("gt", gi)])
                P.op("scalar", lambda e: e.activation(out=tv(gtt[gi], S, T.WB)[:, :, aG:bG], in_=psv(bg, S, wG), func=AF.Identity),
                     reads=[("ps", bg)], writes=[("gt", gi)])

            sl0, wk0 = wload(("up", l, 0))
            wb_ = [next_bank() for _ in range(4)]
            outs = []
            items = []
            for ui, (sl_, wk_) in enumerate(((sl0, wk0),)):
                for jj in range(2):
                    bg = wb_[ui * 4 + jj * 2]
                    bv = wb_[ui * 4 + jj * 2 + 1]
                    outs.append((bg, sl_, wk_, 512, jj, aG, bG))
                    outs.append((bv, sl_, wk_, 512, 2 + jj, aO, bO))
                    items.append((2 * ui + jj, bg, bv, sl_, wk_, 4096 + jj * 384))
            mm_wave(T, outs, hbuf, "h")
            act_warm(AF.Silu)
            for ch, bg, bv, sl_, wk_, dbase in items:
                gi = nxt("gt", NGT)
                gate_evac(gi, bg)
                pend.append((ch, gi, bv, sl_, wk_, dbase))
                while len(pend) > 2:
                    post(pend.pop(0))
            for u in range(1, 11):
                slot, wk = wload(("up", l, u))
                for jj in range(2):
                    ch = 2 * u + jj
                    bg = next_bank()
                    bv = next_bank()
                    mm_dense(T, bg, slot, wk, 512, jj, hbuf, "h", aG, bG)
                    mm_dense(T, bv, slot, wk, 512, 2 + jj, hbuf, "h", aO, bO)
                    gi = nxt("gt", NGT)
                    gate_evac(gi, bg)
                    pend.append((ch, gi, bv, slot, wk, 4096 + jj * 384))
                    while len(pend) > 2:
                        post(pend.pop(0))
            while pend:
                post(pend.pop(0))
            if after_up is not None:
                after_up()
            for oc in range(8):
                slot, wk = wload(("down", l, oc))
                bk = next_bank()

                def mk(k0, k1, slot=slot, bk=bk):
                    def mm(e):
                        ins = None
                        for kc in range(k0, k1):
                            ins = e.matmul(psv(bk, S, L), slot[:, kc * 128:(kc + 1) * 128],
                                           actb[:, kc * 512:kc * 512 + S * L].rearrange("p (s w) -> p s w", s=S),
                                           start=(kc == 0), stop=(kc == FCH - 1))
                        return ins
                    return mm
                cut = FCH - 3 if oc == 0 else FCH
                P.op("tensor", mk(0, cut), reads=[wk] + [("act", kc) for kc in range(cut)], writes=[("ps", bk)])
                if cut < FCH:
                    P.op("tensor", mk(cut, FCH), reads=[wk] + [("act", kc) for kc in range(cut, FCH)], writes=[("ps", bk)])
                resid_add(T, bk, oc, aO, bO, l, cond, 5)

        def load_rope(T, a, b):
            w = b - a
            if T.kind == "p":
                for seg in range(2):
                    for dst, src in ((cosb, cos_d), (sinb, sin_d)):
                        P.op("sync", lambda e, dst=dst, src=src, seg=seg: e.dma_start(
                            out=dst[:, seg * w:(seg + 1) * w], in_=src[:, SEQ_S + (a - 16):SEQ_S + (b - 16)]),
                            writes=["rope"], dma="d_rope")
            else:
                t0 = T.s - 16 + a
                for dst, src in ((cosb, cos_d), (sinb, sin_d)):
                    P.op("sync", lambda e, dst=dst, src=src: e.dma_start(out=dst[:, 0:w], in_=src[:, t0:t0 + w]),
                         writes=["rope"], dma="d_rope")

        def qk_part1(T, bq_, bp_, S, w, gname, pgname, avoid=()):
            i = nxt("sq", NSQ)
            P.op("scalar", lambda e: e.activation(out=tv(sqt[i], S, w), in_=psv(bq_, S, w), func=AF.Square),
                 reads=[("ps", bq_)], writes=[("sq", i)])
            i1 = nxt("tf", NTF)
            i2 = nxt("tf", NTF)
            P.op("vector", lambda e: e.scalar_tensor_tensor(out=tv(tft[i2], S, w), in0=psv(bp_, S, w), scalar=cc(pgname),
                                                            in1=tv(sinb, S, w), op0=ALU.mult, op1=ALU.mult),
                 reads=[("ps", bp_), "cst", "rope"], writes=[("tf", i2)])
            P.op("vector", lambda e: e.scalar_tensor_tensor(out=tv(tft[i1], S, w), in0=psv(bq_, S, w), scalar=cc(gname),
                                                            in1=tv(cosb, S, w), op0=ALU.mult, op1=ALU.mult),
                 reads=[("ps", bq_), "cst", "rope", ("sq", i)], writes=[("tf", i1)])
            P.op("gpsimd", lambda e: e.tensor_tensor(out=tv(tft[i1], S, w), in0=tv(tft[i1], S, w), in1=tv(tft[i2], S, w), op=ALU.add),
                 reads=[("tf", i1), ("tf", i2)], writes=[("tf", i1)])
            bs = next_bank(set(avoid) | {bq_, bp_})
            P.op("tensor", lambda e: e.matmul(psv(bs, S, w), ones_h[:, :], tv(sqt[i], S, w), start=True, stop=True),
                 reads=[("sq", i), "ones"], writes=[("ps", bs)])
            return (bs, i1, S, w)

        def qk_part2(ctx, out_ap, out_key, extra_reads=()):
            bs, i1, S, w = ctx
            rstd_stage(bs, S, w, st_rq)
            P.op("gpsimd", lambda e: e.tensor_tensor(out=out_ap, in0=tv(tft[i1], S, w), in1=tv(st_rq, S, w), op=ALU.mult),
                 reads=[("tf", i1), key_of(st_rq)] + list(extra_reads), writes=list(out_key))

        def qk_norm_rope(T, bq_, bp_, S, w, gname, pgname, out_ap, out_key, extra_reads=(), avoid=()):
            ctx = qk_part1(T, bq_, bp_, S, w, gname, pgname, avoid=avoid)
            qk_part2(ctx, out_ap, out_key, extra_reads)

        def kv_proj(T, l, cond):
            S = T.S
            aO, bO, L = T.aO, T.bO, T.L
            rms_mod(T, aO, bO, l, cond, 0, 1)
            if T.kind == "s":
                src = hbuf[:, :].rearrange("p (c w) -> p c w", c=NCH)[:, :, aO:bO]
                P.op("gpsimd", lambda e, src=src: e.dma_start(out=h1_d[:, :, T.s:T.s + L], in_=src),
                     reads=hkeys(), writes=[("h1", T.ti)], dma="d_h1")
            load_rope(T, aO, bO)
            slot, wk = wload(("kpk",))
            kb_ = [next_bank() for _ in range(4)]
            mm_wave(T, [(kb_[0], slot, wk, 512, 0, aO, bO), (kb_[1], slot, wk, 512, 2, aO, bO),
                        (kb_[2], slot, wk, 512, 1, aO, bO), (kb_[3], slot, wk, 512, 3, aO, bO)], hbuf, "h")
            for g in range(2):
                bq_ = kb_[2 * g]
                bp_ = kb_[2 * g + 1]
                if T.kind == "p":
                    out_ap = knew[:, g * 512:(g + 1) * 512].rearrange("p (s w) -> p s w", s=S)
                    qk_norm_rope(T, bq_, bp_, S, L, "kg", "pkg", out_ap, [("knew", g)])
                    P.op("gpsimd", lambda e, g=g: e.tensor_copy(out=kt_cols(g, KTP, KTP + 512), in_=knew[:, g * 512:(g + 1) * 512]),
                         reads=[("knew", g)], writes=[("KT", g, "p"), ("KTallp", g)])
                    for seg in range(2):
                        q = 2 * T.pt + seg
                        P.op("gpsimd", lambda e, g=g, seg=seg, q=q: e.dma_start(
                            out=nk_d[:, g, q * 256:(q + 1) * 256], in_=knew[:, g * 512 + seg * 256:g * 512 + (seg + 1) * 256]),
                            reads=[("knew", g)], dma="d_nk")
                else:
                    c0 = PAST + T.s
                    out_ap = kt_cols(g, c0, c0 + L).rearrange("p (s w) -> p s w", s=1)
                    qk_norm_rope(T, bq_, bp_, S, L, "kg", "pkg", out_ap, [("KT", g, T.s), ("KTall", g)])
            slot, wk = wload(("v",))
            T.kchunks = []
            if T.kind == "p":
                chunks = [(seg, j * 128, 128) for seg in range(2) for j in range(2)]
            else:
                n4 = [L // 4] * 4
                n4[3] = L - 3 * (L // 4)
                chunks = []
                o = 0
                for n in n4:
                    chunks.append((0, o, n))
                    o += n
            for ci, (seg, o, n) in enumerate(chunks):
                if T.kind == "p":
                    vi = VIP + ci
                    kcol = KTP + seg * 256 + o
                else:
                    vi = 2 + T.ti * 4 + ci
                    kcol = PAST + T.s + o
                bk = next_bank()

                def mm(e, seg=seg, o=o, n=n, bk=bk, slot=slot):
                    ins = None
                    for kc in range(NCH):
                        lhsT = cols(hbuf, kc, T, aO + o, aO + o + n)[:, seg, :]
                        ins = e.matmul(PS[bk // 2][0:n, (bk % 2) * 512:(bk % 2) * 512 + 256], lhsT,
                                       slot[:, kc * 256:(kc + 1) * 256], start=(kc == 0), stop=(kc == NCH - 1))
                    return ins
                P.op("tensor", mm, reads=[wk] + hkeys(), writes=[("ps", bk)])
                P.op("scalar", lambda e, n=n, bk=bk, vi=vi: e.activation(
                    out=Vb[0:n, vi * 256:(vi + 1) * 256], in_=PS[bk // 2][0:n, (bk % 2) * 512:(bk % 2) * 512 + 256], func=AF.Identity),
                    reads=[("ps", bk)], writes=[("V", vi)])
                if T.kind == "p":
                    q = 2 * T.pt + seg
                    vn = vnews[ci % 4]
                    P.op("vector", lambda e, n=n, bk=bk, vn=vn: e.tensor_copy(
                        out=vn[0:n, :], in_=PS[bk // 2][0:n, (bk % 2) * 512:(bk % 2) * 512 + 256]),
                        reads=[("ps", bk)], writes=[("vnew", ci % 4)])
                    P.op("gpsimd", lambda e, q=q, o=o, n=n, vn=vn: e.dma_start(out=nv_d[q * 256 + o:q * 256 + o + n, :], in_=vn[0:n, :]),
                         reads=[("vnew", ci % 4)], dma="d_nv%d" % (ci % 4))
                T.kchunks.append((seg, kcol, n, vi))

        def attention(T, l, cond, key_chunks_by_seg, aQ, bQ, preloaded=False, before_wo=None, after_q=None):
            S = T.S
            wQ = bQ - aQ
            if not preloaded:
                rms_mod(T, aQ, bQ, l, cond, 0, 1)
                load_rope(T, aQ, bQ)
            sl0, wk0 = wload(("q", 0))
            sl1, wk1 = wload(("q", 1))
            wb_ = [next_bank() for _ in range(6)]
            outs = []
            items = []
            for hd in range(3):
                sl_, wk_ = (sl0, wk0) if hd < 2 else (sl1, wk1)
                jj = hd % 2
                bq_ = wb_[2 * hd]
                bp_ = wb_[2 * hd + 1]
                outs.append((bq_, sl_, wk_, 512, jj, aQ, bQ))
                outs.append((bp_, sl_, wk_, 512, 2 + jj, aQ, bQ))
                items.append((hd, bq_, bp_))
            mm_wave(T, outs, hbuf, "h")
            live = {hd: (bq_, bp_) for hd, bq_, bp_ in items}
            slots_q = {0: (sl0, wk0), 1: (sl1, wk1)}

            def avoid():
                return {b for pr in live.values() for b in pr}

            def proj(hd):
                u = hd // 2
                jj = hd % 2
                if u not in slots_q:
                    slots_q[u] = wload(("q", u))
                slot, wk = slots_q[u]
                bq_ = next_bank(avoid())
                bp_ = next_bank(avoid() | {bq_})
                mm_dense(T, bq_, slot, wk, 512, jj, hbuf, "h", aQ, bQ)
                mm_dense(T, bp_, slot, wk, 512, 2 + jj, hbuf, "h", aQ, bQ)
                live[hd] = (bq_, bp_)

            nxt_proj = 3
            pending = None
            for hd in range(8):
                bq_, bp_ = live[hd]
                ctx = qk_part1(T, bq_, bp_, S, wQ, "qg", "pqg", avoid=avoid())
                del live[hd]
                live[("ss", hd)] = (ctx[0],)
                if pending is not None:
                    phd, pctx = pending
                    qk_part2(pctx, cols(ubuf, phd, T, aQ, bQ), [("u", phd)])
                    del live[("ss", phd)]
                pending = (hd, ctx)
                if nxt_proj < 8:
                    proj(nxt_proj)
                    nxt_proj += 1
            phd, pctx = pending
            qk_part2(pctx, cols(ubuf, phd, T, aQ, bQ), [("u", phd)])
            del live[("ss", phd)]
            if after_q is not None:
                after_q()
            ktall = "KTall" + ("p" if T.kind == "p" else "")
            pend = []
            gpi = [0]

            def do_pv(item):
                hd, seg, g, bO_, bD_, pi, pair, si, last = item

                def mm(e):
                    ins = None
                    for j, (kcol, nn, vi) in enumerate(pair):
                        first = (pi == 0 and j == 0)
                        fin = (last and j == len(pair) - 1)
                        e.matmul(bank_ap(bO_)[:, 0:wQ], Vb[0:nn, vi * 256 + g * 128:vi * 256 + (g + 1) * 128],
                                 sgt[si][0:nn, j * 512:j * 512 + wQ], start=first, stop=fin)
                        ins = e.matmul(bank_ap(bD_)[:, 0:wQ], ones_1[0:nn, :],
                                       sgt[si][0:nn, j * 512:j * 512 + wQ], start=first, stop=fin)
                    return ins
                P.op("tensor", mm, reads=[("sg", si), "ones"] + [("V", vi) for (_, _, vi) in pair],
                     writes=[("ps", bO_), ("ps", bD_)])
                if last:
                    P.op("vector", lambda e: e.reciprocal(out=st_t[:, 0:wQ], in_=bank_ap(bD_)[:, 0:wQ]),
                         reads=[("ps", bD_)], writes=[key_of(st_t)])
                    P.op("vector", lambda e: e.tensor_tensor(
                        out=cols(cbuf, hd, T, aQ, bQ)[:, seg, :], in0=bank_ap(bO_)[:, 0:wQ], in1=st_t[:, 0:wQ], op=ALU.mult),
                        reads=[("ps", bO_), key_of(st_t)], writes=[("c", hd)])

            for hd in range(8):
                g = hd // 4
                for seg in range(S):
                    chunks = key_chunks_by_seg[seg]
                    pairs = [chunks[i:i + 2] for i in range(0, len(chunks), 2)]
                    od = (att_ctr[0] % 2) * 2
                    att_ctr[0] += 1
                    bO_ = od
                    bD_ = od + 1
                    qrhs = cols(ubuf, hd, T, aQ, bQ)[:, seg, :]
                    for pi, pair in enumerate(pairs):
                        pb = 4 + 2 * (gpi[0] % 2)
                        gpi[0] += 1
                        n = pair[0][1]

                        def mmq(e, pair=pair, pb=pb, g=g, qrhs=qrhs):
                            ins = None
                            for j, (kcol, nn, vi) in enumerate(pair):
                                ins = e.matmul(PS[pb // 2][0:nn, j * 512:j * 512 + wQ], kt_cols(g, kcol, kcol + nn), qrhs,
                                               start=True, stop=True)
                            return ins
                        P.op("tensor", mmq, reads=[("u", hd), (ktall, g)], writes=[("ps", pb), ("ps", pb + 1)])
                        si = nxt("sg", NSG)
                        np_ = len(pair)
                        P.op("scalar", lambda e, pb=pb, n=n, si=si, np_=np_: e.activation(
                            out=sgt[si][0:n, 0:np_ * 512].rearrange("p (b w) -> p b w", b=np_)[:, :, 0:wQ],
                            in_=PS[pb // 2][0:n, 0:np_ * 512].rearrange("p (b w) -> p b w", b=np_)[:, :, 0:wQ],
                            func=AF.Exp, scale=ATTN_SCALE),
                            reads=[("ps", pb), ("ps", pb + 1)], writes=[("sg", si)])
                        pend.append((hd, seg, g, bO_, bD_, pi, [(k, nn, vi) for (k, nn, vi) in pair], si, pi == len(pairs) - 1))
                        while len(pend) > 1:
                            do_pv(pend.pop(0))
            while pend:
                do_pv(pend.pop(0))
            if before_wo is not None:
                before_wo()
            last_od = ((att_ctr[0] - 1) % 2) * 2
            wo_sl = [wload(("wo", 0)), wload(("wo", 1))]
            wbk = []
            for _ in range(6):
                wbk.append(next_bank({last_od, last_od + 1} | set(wbk)))
            mm_wave(T, [(wbk[oc], wo_sl[oc // 4][0], wo_sl[oc // 4][1], 512, oc % 4, aQ, bQ) for oc in range(6)], cbuf, "c")
            for oc in range(6):
                resid_add(T, wbk[oc], oc, aQ, bQ, l, cond, 2)
            for oc in range(6, 8):
                bk = next_bank()
                mm_dense(T, bk, wo_sl[1][0], wo_sl[1][1], 512, oc % 4, cbuf, "c", aQ, bQ)
                resid_add(T, bk, oc, aQ, bQ, l, cond, 2)

        def final_out(T):
            final_stats(T)
            final_scale(T)

        def final_stats(T):
            S = T.S
            aO, bO, L = T.aO, T.bO, T.L
            bk = next_bank()
            for ch in range(NCH):
                i = nxt("sq", NSQ)
                square_op(ch, tv(sqt[i], S, L), cols(xbuf, ch, T, aO, bO), [("x", ch)], [("sq", i)])
                P.op("tensor", lambda e, ch=ch, i=i: e.matmul(psv(bk, S, L), ones_m[:, :], tv(sqt[i], S, L),
                                                              start=(ch == 0), stop=(ch == NCH - 1)),
                     reads=[("sq", i), "ones"], writes=[("ps", bk)])
            rstd_stage(bk, S, L, st_rstd)

        def final_scale(T):
            S = T.S
            aO, bO, L = T.aO, T.bO, T.L
            for ch in range(NCH):
                i = nxt("tf", NTF)
                P.op("vector", lambda e, ch=ch, i=i: e.scalar_tensor_tensor(
                    out=tv(tft[i], S, L), in0=cols(xbuf, ch, T, aO, bO), scalar=cc("fg", ch),
                    in1=tv(st_rstd, S, L), op0=ALU.mult, op1=ALU.mult),
                    reads=[("x", ch), key_of(st_rstd), "cst"], writes=[("tf", i)])
                if T.kind == "p":
                    q0 = 2 * T.pt
                    dst = yp_d[:, ch, q0 * 256:q0 * 256 + 512]
                else:
                    dst = ys_d[:, ch, T.s:T.s + L]
                P.op("gpsimd", lambda e, dst=dst, i=i: e.dma_start(out=dst, in_=tft[i][:, 0:S * L]),
                     reads=[("tf", i)], dma="d_y%d" % i)

        tiles = make_tiles()
        ptiles = [t for t in tiles if t.kind == "p"]
        stiles = [t for t in tiles if t.kind == "s"]
        for i, t in enumerate(stiles):
            t.ti = i
        emit_mods(0)
        emit_cache()
        for T in stiles:
            load_x(T)
            conv_mixer(T, 0, 0)
            ffn(T, 0, 0, T.aM, T.bM)
            for ch in range(NCH):
                P.op("gpsimd", lambda e, ch=ch, T=T: e.dma_start(out=x1_d[:, ch, T.s:T.s + T.L], in_=xview(ch, T.aO, T.bO)),
                     reads=[("x", ch)], writes=[("x1", T.ti, ch)], dma="d_s%d" % ch)
            if T.ti == 0:
                emit_mods(1)
            kv_proj(T, 1, 0)
        for T in ptiles:
            load_x(T)
            conv_mixer(T, 0, 1)
            ffn(T, 0, 1, T.aM, T.bM)
            kv_proj(T, 1, 1)
            kc = {seg: [(k, n, vi) for (sg_, k, n, vi) in T.kchunks if sg_ == seg] for seg in range(2)}
            P.op("gpsimd", lambda e: e.memset(dummy[:, 0:1], 0.0), reads=[("KT", 0, "p"), ("KT", 1, "p")],
                 writes=[("KTallp", 0), ("KTallp", 1)])
            attention(T, 1, 1, kc, T.aO, T.bO, preloaded=True)
            ffn(T, 1, 1, T.aO, T.bO)
            final_out(T)
        allk = [("KT", g, "c") for g in range(2)] + [("KT", g, T.s) for g in range(2) for T in stiles]
        P.op("gpsimd", lambda e: e.memset(dummy[:, 0:1], 0.0), reads=allk, writes=[("KTall", 0), ("KTall", 1)])
        chunks = [(0, 128, 0), (128, 128, 1)]
        for T in stiles:
            chunks += [(k, n, vi) for (_, k, n, vi) in T.kchunks]
        def prefetch_B(T):
            w = T.bM - T.aM
            t0 = T.s - 16 + T.aM
            dst = hbuf[:, :].rearrange("p (c w) -> p c w", c=NCH)[:, :, T.aM:T.bM]
            P.op("sync", lambda e, dst=dst, t0=t0, w=w: e.dma_start(out=dst, in_=h1_d[:, :, t0:t0 + w]),
                 reads=[("h1", i) for i in range(len(stiles))], writes=hkeys(), dma="d_hl")
            load_rope(T, T.aM, T.bM)

        def reload_x(T):
            w = T.bM - T.aM
            t0 = T.s - 16 + T.aM
            for ch in range(NCH):
                P.op("sync", lambda e, ch=ch, T=T, t0=t0, w=w: e.dma_start(out=xview(ch, T.aM, T.bM), in_=x1_d[:, ch, t0:t0 + w]),
                     reads=[("x1", i, ch) for i in range(len(stiles))], writes=[("x", ch)], dma="d_x%d" % ch)

        prevT = None
        for idx, T in enumerate(stiles):
            if idx == 0:
                prefetch_B(T)
            attention(T, 1, 0, {0: chunks}, T.aM, T.bM, preloaded=True, before_wo=(lambda T=T: reload_x(T)),
                      after_q=(lambda pT=prevT: final_scale(pT)) if prevT is not None else None)
            ffn(T, 1, 0, T.aM, T.bM)
            if idx + 1 < len(stiles):
                prefetch_B(stiles[idx + 1])
            final_stats(T)
            prevT = T
        final_scale(prevT)
        P.emit(sems)
    return nc, order, offs, wtotal


_CACHE = {}


def _rope_tables():
    t = np.arange(SEQ_S)
    row = (t // 64).astype(np.float32)
    col = (t % 64).astype(np.float32)
    freqs = (10000.0 ** (-np.arange(0, 64, 2, dtype=np.float32) / 64)).astype(np.float32)
    cosT = np.ones((128, SEQ_S + SEQ_P), np.float32)
    sinT = np.zeros((128, SEQ_S + SEQ_P), np.float32)
    for d in range(128):
        pos = row if d < 64 else col
        dd = d % 64
        ang = (pos * freqs[dd % 32]).astype(np.float32)
        cosT[d, :SEQ_S] = np.cos(ang)
        sgn = -1.0 if dd < 32 else 1.0
        sinT[d, :SEQ_S] = sgn * np.sin(ang)
    return cosT, sinT


def kernel(**inputs):
    I = {k: np.asarray(v) for k, v in inputs.items()}
    if "prog" not in _CACHE:
        _CACHE["prog"] = build_program()
    nc, order, offs, wtotal = _CACHE["prog"]
    n = 8
    wf = np.empty((wtotal,), np.float32)
    for name in order:
        off, E, rec = offs[name]
        a = rec(I)
        assert a.shape == (128, E), (name, a.shape, E)
        wf[off:off + 128 * E] = a.reshape(-1)
    cosT, sinT = _rope_tables()

    def fm(v):
        return np.ascontiguousarray(v.reshape(-1, 128).T)

    cstv = np.zeros((128, NCONST), np.float32)

    def put(name, arr):
        c0, nn = CMAP[name]
        assert arr.shape == (128, nn), (name, arr.shape, nn)
        cstv[:, c0:c0 + nn] = arr
    adab = np.stack([fm(I["ada_b"][l]) for l in range(2)], axis=1)
    put("adab", np.repeat(adab[:, :, :, None], 2, axis=3).reshape(128, 192))
    put("n1g", np.stack([fm(I["norm1_g"][l]) for l in range(2)], axis=1).reshape(128, 16))
    put("n2g", np.stack([fm(I["norm2_g"][l]) for l in range(2)], axis=1).reshape(128, 16))
    put("fg", fm(I["final_g"]))
    put("dwb", fm(I["conv_dw_b"][0]))
    put("lng", fm(I["conv_ln_g"][0]))
    put("lnb", fm(I["conv_ln_b"][0]))
    put("fcb", np.stack([fm(I["ffn_conv_b"][l]) for l in range(2)], axis=1).reshape(128, 2 * FCH))
    put("qg", I["attn_q_g"][0].reshape(128, 1))
    put("pqg", I["attn_q_g"][0][PERM].reshape(128, 1))
    put("kg", I["attn_k_g"][0].reshape(128, 1))
    put("pkg", I["attn_k_g"][0][PERM].reshape(128, 1))
    put("eps", np.full((128, 1), EPS, np.float32))

    in_maps = []
    for c in range(n):
        xs = np.ascontiguousarray(I["x_sample"][c].reshape(SEQ_S, NCH, 128).transpose(2, 1, 0))
        xp = np.ascontiguousarray(I["x_prompt"][4 * c:4 * c + 4].reshape(4 * SEQ_P, NCH, 128).transpose(2, 1, 0))
        cond = np.stack([fm(I["c"][c]), fm(I["c_ctx"])], axis=2).reshape(128, 16)
        ck = I["cache_attn_k"][c, 0]
        ckT = np.ascontiguousarray(ck.transpose(2, 1, 0)).reshape(128, 512)
        cvv = I["cache_attn_v"][c, 0].reshape(2, 128, 256)
        cv = np.ascontiguousarray(cvv.transpose(1, 0, 2)).reshape(128, 512)
        in_maps.append({"xs": xs, "xp": xp, "cond": np.ascontiguousarray(cond), "ckT": ckT, "cv": cv,
                        "cst": cstv, "cosT": cosT, "sinT": sinT, "wf": wf})
    res = run_bass_kernel_spmd(nc, in_maps, core_ids=list(range(n)))
    if DEBUG:
        _CACHE["dbg"] = res.results[0]
    y_prompt = np.empty((32, SEQ_P, D), np.float32)
    y_sample = np.empty((8, SEQ_S, D), np.float32)
    new_k = np.empty((32, 1, SEQ_P, 2, HD), np.float32)
    new_v = np.empty((32, 1, SEQ_P, 2, HD), np.float32)
    for c in range(n):
        r = res.results[c]
        y_sample[c] = r["ys"].transpose(2, 1, 0).reshape(SEQ_S, D)
        y_prompt[4 * c:4 * c + 4] = r["yp"].transpose(2, 1, 0).reshape(4, SEQ_P, D)
        new_k[4 * c:4 * c + 4, 0] = r["nk"].reshape(128, 2, 4, SEQ_P).transpose(2, 3, 1, 0)
        new_v[4 * c:4 * c + 4, 0] = r["nv"].reshape(4, SEQ_P, 2, HD)
    return (y_prompt, y_sample, new_k, new_v)
```

```python
import contextlib
import numpy as np
import concourse.bass as bass
import concourse.mybir as mybir
from concourse.bass_utils import run_bass_kernel_spmd

F32 = mybir.dt.float32
BF16 = mybir.dt.bfloat16
AF = mybir.ActivationFunctionType
ALU = mybir.AluOpType

ENGINES = ("tensor", "vector", "scalar", "gpsimd", "sync")

D = 1024
NCH = 8
FFN = 2816
FCH = 22
HD = 128
EPS = 1e-6
ATTN_SCALE = HD ** -0.5
SEQ_S = 4096
SEQ_P = 256
PAST = 256
WBMAX = 576
SLOT_E = 4864
NSLOT = 4
TS = 456


class Op:
    __slots__ = ("eng", "fn", "deps", "token_sem", "token_val", "signal", "is_dma", "dma_sem")


class Prog:
    def __init__(self, nc):
        self.nc = nc
        self.ops = []
        self.per_eng = {e: [] for e in ENGINES}
        self.last_writer = {}
        self.readers = {}

    def op(self, eng, fn, reads=(), writes=(), dma=None):
        o = Op()
        o.eng = eng
        o.fn = fn
        o.is_dma = dma is not None
        o.dma_sem = dma
        o.signal = o.is_dma
        deps = set()
        for k in reads:
            w = self.last_writer.get(k)
            if w is not None:
                deps.add(w)
        for k in writes:
            w = self.last_writer.get(k)
            if w is not None:
                deps.add(w)
            for r in self.readers.get(k, ()):
                deps.add(r)
        keep = []
        rset = set(reads)
        for d in deps:
            if d.eng == eng and not d.is_dma and not o.is_dma:
                if eng == "tensor":
                    continue
                raw = any(self.last_writer.get(k) is d for k in rset)
                if not raw:
                    continue
            keep.append(d)
        o.deps = keep
        for d in keep:
            d.signal = True
        for k in reads:
            self.readers.setdefault(k, []).append(o)
        for k in writes:
            self.last_writer[k] = o
            self.readers[k] = []
        self.ops.append(o)
        self.per_eng[eng].append(o)
        return o

    def emit(self, sem_ctx):
        nc = self.nc
        for e in ENGINES:
            cnt = 0
            for o in self.per_eng[e]:
                if not o.is_dma and o.signal:
                    cnt += 1
                    o.token_sem = sem_ctx["eng_" + e]
                    o.token_val = cnt
        dc = {}
        for o in self.ops:
            if o.is_dma:
                dc[o.dma_sem] = dc.get(o.dma_sem, 0) + 16
                o.token_sem = sem_ctx[o.dma_sem]
                o.token_val = dc[o.dma_sem]

        def run(e, eng):
            known = {}
            for o in self.per_eng[e]:
                waits = {}
                for d in o.deps:
                    key = id(d.token_sem)
                    if known.get(key, 0) >= d.token_val:
                        continue
                    if key not in waits or waits[key][1] < d.token_val:
                        waits[key] = (d.token_sem, d.token_val)
                for key, (s, v) in waits.items():
                    eng.wait_ge(s, v)
                    known[key] = v
                ins = o.fn(eng)
                if o.is_dma:
                    ins.then_inc(o.token_sem, 16)
                elif o.signal:
                    ins.then_inc(o.token_sem, 1)
            if e == "sync":
                for name, v in dc.items():
                    eng.wait_ge(sem_ctx[name], v)

        with nc.Block() as block:
            @block.tensor
            def _(eng):
                run("tensor", eng)

            @block.vector
            def _(eng):
                run("vector", eng)

            @block.scalar
            def _(eng):
                run("scalar", eng)

            @block.gpsimd
            def _(eng):
                run("gpsimd", eng)

            @block.sync
            def _(eng):
                run("sync", eng)


def _perm128():
    p = np.arange(128)
    q = p % 64
    return (p // 64) * 64 + np.where(q < 32, q + 32, q - 32)


PERM = _perm128()


def _kc(W, cols):
    K = W.shape[0]
    a = W[:, cols].reshape(K // 128, 128, len(cols))
    return np.ascontiguousarray(a.transpose(1, 0, 2)).reshape(128, -1)


def _diag(wtaps, ch):
    nt = wtaps.shape[0]
    out = np.zeros((128, nt, 128), np.float32)
    idx = np.arange(128)
    out[idx, :, idx] = wtaps[:, ch * 128:(ch + 1) * 128].T
    return out.reshape(128, -1)


def _r(a, b):
    return np.arange(a, b)


def unit_catalogue():
    cat = []

    def ada_units(l):
        for j in range(24):
            cat.append((("ada", l, j), 2048,
                        lambda I, l=l, j=j: _kc(I["ada_w"][l], _r(j * 256, j * 256 + 256))))
    ada_units(0)

    def phaseA_units():
        for u in range(4):
            def rec(I, u=u):
                cols = np.concatenate([_r(2 * u * 128, (2 * u + 2) * 128),
                                       _r(1024 + 2 * u * 128, 1024 + (2 * u + 2) * 128)])
                return _kc(I["conv_w_pw1"][0], cols)
            cat.append((("pw1", u), 4096, rec))
            for ch in (2 * u, 2 * u + 1):
                cat.append((("dg31", ch), 31 * 128, lambda I, ch=ch: _diag(I["conv_dw_w"][0], ch)))
        for u in range(2):
            cat.append((("pw2", u), 4096, lambda I, u=u: _kc(I["conv_w_pw2"][0], _r(u * 512, u * 512 + 512))))
        ffn_units(0)
        ada_units(1)
        def reck(I):
            cols = np.concatenate([_r(1024, 1280), 1024 + PERM, 1152 + PERM])
            return _kc(I["attn_w_qkv"][0], cols)
        cat.append((("kpk",), 4096, reck))
        cat.append((("v",), 2048, lambda I: _kc(I["attn_w_qkv"][0], _r(1280, 1536))))

    def ffn_units(l):
        for u in range(11):
            def rec(I, u=u, l=l):
                cols = np.concatenate([_r(2 * u * 128, (2 * u + 2) * 128),
                                       _r(FFN + 2 * u * 128, FFN + (2 * u + 2) * 128)])
                a = _kc(I["ffn_w_up"][l], cols)
                d0 = _diag(I["ffn_conv_w"][l], 2 * u)
                d1 = _diag(I["ffn_conv_w"][l], 2 * u + 1)
                return np.concatenate([a, d0, d1], axis=1)
            cat.append((("up", l, u), 4864, rec))
        for oc in range(8):
            cat.append((("down", l, oc), FCH * 128,
                        lambda I, oc=oc, l=l: _kc(I["ffn_w_down"][l], _r(oc * 128, oc * 128 + 128))))

    def phaseB_units():
        for u in range(4):
            def rec(I, u=u):
                h0, h1 = 2 * u, 2 * u + 1
                cols = np.concatenate([_r(h0 * 128, h0 * 128 + 256), h0 * 128 + PERM, h1 * 128 + PERM])
                return _kc(I["attn_w_qkv"][0], cols)
            cat.append((("q", u), 4096, rec))
        for u in range(2):
            cat.append((("wo", u), 4096, lambda I, u=u: _kc(I["attn_w_o"][0], _r(u * 512, u * 512 + 512))))
        ffn_units(1)

    phaseA_units()
    phaseB_units()
    offs = {}
    off = 0
    order = []
    for name, E, rec in cat:
        offs[name] = (off, E, rec)
        order.append(name)
        off += 128 * E
    return order, offs, off


def const_map():
    m = {}
    c = 0

    def add(name, n):
        nonlocal c
        m[name] = (c, n)
        c += n
    add("adab", 2 * 96)
    add("n1g", 16)
    add("n2g", 16)
    add("fg", 8)
    add("dwb", 8)
    add("lng", 8)
    add("lnb", 8)
    add("fcb", 2 * FCH)
    add("qg", 1)
    add("pqg", 1)
    add("kg", 1)
    add("pkg", 1)
    add("eps", 1)
    return m, c


CMAP, NCONST = const_map()


class Tile:
    pass


def make_tiles():
    tiles = []
    for pt in range(2):
        t = Tile()
        t.kind = "p"
        t.S = 2
        t.L = 256
        t.WB = 288
        t.aE, t.bE = 16, 272
        t.aM, t.bM = 16, 272
        t.s = 0
        t.pt = pt
        t.cond = 1
        tiles.append(t)
    s = 0
    while s < SEQ_S:
        e = min(s + TS, SEQ_S)
        t = Tile()
        t.kind = "s"
        t.S = 1
        t.L = e - s
        t.WB = t.L + 32
        lo = max(s - 16, 0)
        hi = min(e + 16, SEQ_S)
        t.aE, t.bE = lo - (s - 16), hi - (s - 16)
        lo = max(s - 1, 0)
        hi = min(e + 1, SEQ_S)
        t.aM, t.bM = lo - (s - 16), hi - (s - 16)
        t.s = s
        t.cond = 0
        tiles.append(t)
        s = e
    for t in tiles:
        t.aO, t.bO = 16, 16 + t.L
    return tiles


DEBUG = False
USE_LN = False


def build_program():
    nc = bass.Bass("TRN2", target_bir_lowering=False)
    order, offs, wtotal = unit_catalogue()

    def din(name, shape, dt=F32):
        return nc.dram_tensor(name, shape, dt, kind="ExternalInput").ap()

    def dout(name, shape, dt=F32):
        return nc.dram_tensor(name, shape, dt, kind="ExternalOutput").ap()

    xs_d = din("xs", [128, NCH, SEQ_S])
    xp_d = din("xp", [128, NCH, 4 * SEQ_P])
    cond_d = din("cond", [128, 16])
    ckT_d = din("ckT", [128, 2 * PAST])
    cv_d = din("cv", [128, 2 * 256])
    cst_d = din("cst", [128, NCONST])
    cos_d = din("cosT", [128, SEQ_S + SEQ_P])
    sin_d = din("sinT", [128, SEQ_S + SEQ_P])
    wf_d = din("wf", [wtotal])
    ys_d = dout("ys", [128, NCH, SEQ_S])
    yp_d = dout("yp", [128, NCH, 4 * SEQ_P])
    nk_d = dout("nk", [128, 2, 4 * SEQ_P])
    nv_d = dout("nv", [4 * SEQ_P, 256])
    if DEBUG:
        dbg_c = dout("dbg_c", [128, NCH * WBMAX], BF16)
        dbg_x = dout("dbg_x", [128, NCH * WBMAX])
        dbg_q = dout("dbg_q", [128, NCH * WBMAX], BF16)
        dbg_k = dout("dbg_k", [128, 2 * (SEQ_S + PAST)], BF16)
        dbg_v = dout("dbg_v", [128, 40 * 256], BF16)
    wb_d = nc.dram_tensor("wb", [wtotal], BF16, kind="Internal").ap()
    x1_d = nc.dram_tensor("x1s", [128, NCH, SEQ_S], F32, kind="Internal").ap()
    h1_d = nc.dram_tensor("h1s", [128, NCH, SEQ_S], BF16, kind="Internal").ap()

    sem_names = ["eng_" + e for e in ENGINES] + ["d_const", "d_rope", "d_nk", "d_h1", "d_hl"] + ["d_nv%d" % i for i in range(4)] + ["d_x%d" % i for i in range(NCH)] + ["d_s%d" % i for i in range(NCH)] + ["d_y%d" % i for i in range(4)] + \
                ["d_cvt%d" % i for i in range(8)] + ["d_w%d" % i for i in range(NSLOT)]

    with contextlib.ExitStack() as st:
        sems = {n: st.enter_context(nc.semaphore(n)) for n in sem_names}

        def sb(name, cols, dt):
            return st.enter_context(nc.sbuf_tensor("s_" + name, [128, cols], dt))

        xbuf = sb("xbuf", NCH * WBMAX, F32)
        hbuf = sb("hbuf", NCH * WBMAX, BF16)
        ubuf = sb("ubuf", NCH * WBMAX, BF16)
        cbuf = sb("cbuf", NCH * WBMAX, BF16)
        actb = sb("actb", FCH * 512, BF16)
        KT = sb("KT", 2 * (SEQ_S + PAST + 512), BF16)
        NVCH = 42
        Vb = sb("Vb", NVCH * 256, BF16)
        slots = [sb("wslot%d" % i, SLOT_E, BF16) for i in range(NSLOT)]
        NSQ = 4
        sqt = [sb("sqt%d" % i, WBMAX, BF16) for i in range(NSQ)]
        NTF = 4
        tft = [sb("tft%d" % i, WBMAX, F32) for i in range(NTF)]
        NSG = 3
        sgt = [sb("sgt%d" % i, 1152, BF16) for i in range(NSG)]
        NGT = 4
        gtt = [sb("gtt%d" % i, WBMAX, BF16) for i in range(NGT)]
        slt = [sb("slt%d" % i, 512, BF16) for i in range(NGT)]
        st_mean = sb("st_mean", WBMAX, F32)
        st_rstd = sb("st_rstd", WBMAX, F32)
        st_nmr = sb("st_nmr", WBMAX, F32)
        st_t = sb("st_t", WBMAX, F32)
        st_rq = sb("st_rq", WBMAX, F32)
        cosb = sb("cosb", 512, F32)
        sinb = sb("sinb", 512, F32)
        knew = sb("knew", 2 * 512, F32)
        vnews = [sb("vnew%d" % i, 256, F32) for i in range(4)]
        cst = sb("cst", NCONST, F32)
        condb = sb("condb", 16, F32)
        scondf = sb("scondf", 16, F32)
        modb = sb("modb", 2 * 96, F32)
        der = sb("der", 2 * 2 * 6 * 8, F32)
        ckf = sb("ckf", 512, F32)
        cvf = sb("cvf", 512, F32)
        dummy = sb("dummy", 8, F32)
        ones_m = sb("ones_m", 128, BF16)
        ones_h = sb("ones_h", 128, BF16)
        ones_1 = sb("ones_1", 128, BF16)
        PS = [st.enter_context(nc.psum_tensor("ps%d" % i, [128, 1024], F32)) for i in range(4)]

        P = Prog(nc)

        def bank_ap(b):
            return PS[b // 2][:, (b % 2) * 512:(b % 2) * 512 + 512]

        def psv(b, S, w, n=128):
            return PS[b // 2][0:n, (b % 2) * 512:(b % 2) * 512 + S * w].rearrange("p (s w) -> p s w", s=S)

        def cols(buf, ch, T, a, b, stride=WBMAX):
            base = ch * stride
            return buf[:, base:base + T.S * T.WB].rearrange("p (s w) -> p s w", s=T.S)[:, :, a:b]

        def tv(tbuf, S, w):
            return tbuf[:, 0:S * w].rearrange("p (s w) -> p s w", s=S)

        def cc(name, i=0, n=1):
            c0, _ = CMAP[name]
            return cst[:, c0 + i:c0 + i + n]

        bank_ctr = [0]
        att_ctr = [0]

        def next_bank(avoid=()):
            assert len(set(avoid)) < 8
            while True:
                b = bank_ctr[0] % 8
                bank_ctr[0] += 1
                if b not in avoid:
                    return b

        rot = {"sq": 0, "tf": 0, "sg": 0, "gt": 0}

        def nxt(k, n):
            v = rot[k] % n
            rot[k] += 1
            return v

        wctr = [0]

        def wload(name):
            off, E, _ = offs[name]
            s = wctr[0] % NSLOT
            wctr[0] += 1
            src = wb_d[off:off + 128 * E].rearrange("(p e) -> p e", p=128)
            P.op("sync", lambda e: e.dma_start(out=slots[s][:, 0:E], in_=src),
                 reads=[("wb", name)], writes=[("ws", s)], dma="d_w%d" % s)
            return slots[s], ("ws", s)

        def wload_f32(name):
            off, E, _ = offs[name]
            s = wctr[0] % NSLOT
            wctr[0] += 1
            src = wf_d[off:off + 128 * E].rearrange("(p e) -> p e", p=128)
            dstf = slots[s][:, 0:2 * E].bitcast(F32)
            P.op("sync", lambda e: e.dma_start(out=dstf, in_=src), reads=[], writes=[("ws", s)], dma="d_w%d" % s)
            return dstf, ("ws", s)

        def wblk(slot, kc, j, ncols=512, w=128):
            base = kc * ncols + j * 128
            return slot[:, base:base + w]

        for name in order:
            off, E, _ = offs[name]
            if name[0] == "ada":
                continue
            P.op("gpsimd", lambda e, off=off, E=E: e.dma_start(out=wb_d[off:off + 128 * E].rearrange("(p e) -> p e", p=128),
                                                               in_=wf_d[off:off + 128 * E].rearrange("(p e) -> p e", p=128)),
                 reads=[], writes=[("wb", name)], dma="d_cvt%d" % (order.index(name) % 8))
        P.op("sync", lambda e: e.dma_start(out=cst[:, :], in_=cst_d), writes=["cst"], dma="d_const")
        P.op("sync", lambda e: e.dma_start(out=condb[:, :], in_=cond_d), writes=["cst"], dma="d_const")
        P.op("sync", lambda e: e.dma_start(out=ckf[:, :], in_=ckT_d), writes=["cst"], dma="d_const")
        P.op("sync", lambda e: e.dma_start(out=cvf[:, :], in_=cv_d), writes=["cst"], dma="d_const")
        P.op("vector", lambda e: e.memset(ones_m[:, :], 1.0 / 1024), writes=["ones"])
        P.op("vector", lambda e: e.memset(ones_h[:, :], 1.0 / 128), writes=["ones"])
        P.op("vector", lambda e: e.memset(ones_1[:, :], 1.0), writes=["ones"])
        P.op("vector", lambda e: e.memset(dummy[:, :], 1.0), writes=["dummy_in"])
        P.op("scalar", lambda e: e.activation(out=scondf[:, :], in_=condb[:, :], func=AF.Silu),
             reads=["cst"], writes=["scond"])

        def emit_mods(l):
            b = next_bank()
            for j in range(24):
                slotf, wk = wload_f32(("ada", l, j))
                for jj in range(2):
                    m = j * 2 + jj

                    def mm(e, slotf=slotf, jj=jj, m=m, b=b):
                        ins = None
                        for kc in range(8):
                            ins = e.matmul(bank_ap(b)[:, 2 * m:2 * m + 2], slotf[:, kc * 256 + jj * 128:kc * 256 + (jj + 1) * 128],
                                           scondf[:, 2 * kc:2 * kc + 2], start=(kc == 0), stop=(kc == 7))
                        return ins
                    P.op("tensor", mm, reads=[wk, "scond"], writes=[("ps", b)])
            P.op("vector", lambda e, l=l, b=b: e.tensor_tensor(out=modb[:, l * 96:(l + 1) * 96], in0=bank_ap(b)[:, 0:96],
                                                               in1=cc("adab", l * 96, 96), op=ALU.add),
                 reads=[("ps", b), "cst"], writes=[("modb", l)])
            emit_der(l)

        def modv(l, which, cond):
            base = l * 96 + which * 16
            return modb[:, base:base + 16].rearrange("p (c t) -> p c t", t=2)[:, :, cond:cond + 1]

        def derv(l, cond, k, ch=None):
            base = ((l * 2 + cond) * 6 + k) * 8
            if ch is None:
                return der[:, base:base + 8]
            return der[:, base + ch:base + ch + 1]

        def emit_der(l):
            for cond in range(2):
                for k, (which, gname) in enumerate([(1, "n1g"), (0, None), (2, None), (4, "n2g"), (3, None), (5, None)]):
                    dst = derv(l, cond, k).rearrange("p (c t) -> p c t", t=1)
                    src = modv(l, which, cond)
                    if gname is not None:
                        g = cc(gname, l * 8, 8).rearrange("p (c t) -> p c t", t=1)
                        P.op("vector", lambda e, dst=dst, src=src, g=g: e.scalar_tensor_tensor(
                            out=dst, in0=src, scalar=1.0, in1=g, op0=ALU.add, op1=ALU.mult),
                            reads=[("modb", l), "cst"], writes=[("der", l)])
                    else:
                        P.op("vector", lambda e, dst=dst, src=src: e.tensor_copy(out=dst, in_=src),
                             reads=[("modb", l)], writes=[("der", l)])

        KTW = SEQ_S + PAST + 512
        KTP = SEQ_S + PAST
        VIP = 38

        def kt_cols(g, a, b):
            return KT[:, g * KTW + a:g * KTW + b]

        def emit_cache():
            for g in range(2):
                P.op("vector", lambda e, g=g: e.tensor_copy(out=kt_cols(g, 0, PAST), in_=ckf[:, g * PAST:(g + 1) * PAST]),
                     reads=["cst"], writes=[("KT", g, "c"), ("KTall", g)])
            for j in range(2):
                P.op("vector", lambda e, j=j: e.tensor_copy(out=Vb[:, j * 256:(j + 1) * 256], in_=cvf[:, j * 256:(j + 1) * 256]),
                     reads=["cst"], writes=[("V", j)])

        def xkeys():
            return [("x", ch) for ch in range(NCH)]

        def hkeys():
            return [("h", ch) for ch in range(NCH)]

        def xview(ch, c0, c1):
            return xbuf[:, ch * WBMAX + c0:ch * WBMAX + c1]

        def load_x(T):
            for ch in range(NCH):
                if T.kind == "p":
                    for seg in range(2):
                        q = 2 * T.pt + seg
                        dst = xview(ch, seg * T.WB + 16, seg * T.WB + 16 + 256)
                        src = xp_d[:, ch, q * 256:(q + 1) * 256]
                        P.op("sync", lambda e, dst=dst, src=src: e.dma_start(out=dst, in_=src), writes=[("x", ch)], dma="d_x%d" % ch)
                else:
                    w = T.bE - T.aE
                    t0 = T.s - 16 + T.aE
                    dst = xview(ch, T.aE, T.bE)
                    src = xs_d[:, ch, t0:t0 + w]
                    P.op("sync", lambda e, dst=dst, src=src: e.dma_start(out=dst, in_=src), writes=[("x", ch)], dma="d_x%d" % ch)

        def key_of(buf):
            return id(buf)

        def rstd_stage(b, S, w, dst):
            if USE_LN:
                P.op("scalar", lambda e: e.activation(out=tv(dst, S, w), in_=psv(b, S, w), func=AF.Ln,
                                                      bias=cc("eps"), scale=1.0),
                     reads=[("ps", b), "cst"], writes=[key_of(dst)])
                P.op("scalar", lambda e: e.activation(out=tv(dst, S, w), in_=tv(dst, S, w), func=AF.Exp, scale=-0.5),
                     reads=[key_of(dst)], writes=[key_of(dst)])
            else:
                P.op("scalar", lambda e: e.activation(out=tv(dst, S, w), in_=psv(b, S, w), func=AF.Sqrt,
                                                      bias=cc("eps"), scale=1.0),
                     reads=[("ps", b), "cst"], writes=[key_of(dst)])
                P.op("vector", lambda e: e.reciprocal(out=tv(dst, S, w), in_=tv(dst, S, w)),
                     reads=[key_of(dst)], writes=[key_of(dst)])

        def square_op(idx, out_ap, in_ap, rkeys, wkeys):
            m = idx % 8
            if m in (0, 3, 6):
                P.op("gpsimd", lambda e: e.tensor_tensor(out=out_ap, in0=in_ap, in1=in_ap, op=ALU.mult), reads=rkeys, writes=wkeys)
            else:
                P.op("scalar", lambda e: e.activation(out=out_ap, in_=in_ap, func=AF.Square), reads=rkeys, writes=wkeys)

        def rms_mod(T, a, b, l, cond, kA, kB):
            S = T.S
            w = b - a
            bk = next_bank()
            act_warm(AF.Sqrt)
            for ch in range(NCH):
                i = nxt("sq", NSQ)
                square_op(ch, tv(sqt[i], S, w), cols(xbuf, ch, T, a, b), [("x", ch)], [("sq", i)])
                P.op("tensor", lambda e, ch=ch, i=i: e.matmul(psv(bk, S, w), ones_m[:, :], tv(sqt[i], S, w),
                                                              start=(ch == 0), stop=(ch == NCH - 1)),
                     reads=[("sq", i), "ones"], writes=[("ps", bk)])
            rstd_stage(bk, S, w, st_rstd)
            for ch in range(NCH):
                i = nxt("tf", NTF)
                P.op("vector", lambda e, ch=ch, i=i: e.scalar_tensor_tensor(
                    out=tv(tft[i], S, w), in0=cols(xbuf, ch, T, a, b), scalar=derv(l, cond, kA, ch),
                    in1=tv(st_rstd, S, w), op0=ALU.mult, op1=ALU.mult),
                    reads=[("x", ch), key_of(st_rstd), ("der", l)], writes=[("tf", i)])
                P.op("scalar", lambda e, ch=ch, i=i: e.activation(
                    out=cols(hbuf, ch, T, a, b), in_=tv(tft[i], S, w), func=AF.Identity,
                    bias=derv(l, cond, kB, ch), scale=1.0),
                    reads=[("tf", i), ("der", l)], writes=[("h", ch)])

        def resid_add(T, bk, oc, a, b, l, cond, kG):
            S = T.S
            w = b - a
            P.op("vector", lambda e: e.scalar_tensor_tensor(
                out=cols(xbuf, oc, T, a, b), in0=psv(bk, S, w), scalar=derv(l, cond, kG, oc),
                in1=cols(xbuf, oc, T, a, b), op0=ALU.mult, op1=ALU.add),
                reads=[("ps", bk), ("x", oc), ("der", l)], writes=[("x", oc)])

        def mm_dense(T, bk, slot, wk, ncols, j, src_buf, src_key, a, b, nk=NCH, w128=128, src_stride=WBMAX, src_cols=None):
            S = T.S
            w = b - a

            def mm(e):
                ins = None
                for kc in range(nk):
                    rhs = cols(src_buf, kc, T, a, b, stride=src_stride) if src_cols is None else src_cols(kc)
                    ins = e.matmul(psv(bk, S, w), wblk(slot, kc, j, ncols), rhs, start=(kc == 0), stop=(kc == nk - 1))
                return ins
            P.op("tensor", mm, reads=[wk] + [(src_key, kc) for kc in range(nk)], writes=[("ps", bk)])

        def mm_wave(T, outs, src_buf, src_key, nk=NCH):
            S = T.S
            wks = list({o[2] for o in outs})
            for kc in range(nk):
                def mm(e, kc=kc):
                    ins = None
                    for (bk, slot, wk, ncols, j, a, b) in outs:
                        ins = e.matmul(psv(bk, S, b - a), wblk(slot, kc, j, ncols), cols(src_buf, kc, T, a, b),
                                       start=(kc == 0), stop=(kc == nk - 1))
                    return ins
                P.op("tensor", mm, reads=wks + [(src_key, kc)], writes=[("ps", o[0]) for o in outs])

        def act_warm(func):
            P.op("scalar", lambda e: e.activation(out=dummy[:, 2:3], in_=dummy[:, 4:5], func=func), reads=["dummy_in"], writes=["dummy_out"])

        def conv_mixer(T, l, cond):
            S = T.S
            aE, bE, aM, bM = T.aE, T.bE, T.aM, T.bM
            wE = bE - aE
            wM = bM - aM
            rms_mod(T, aE, bE, l, cond, 0, 1)
            if aE > 0:
                for ch in range(NCH):
                    P.op("gpsimd", lambda e, ch=ch: e.memset(cols(ubuf, ch, T, 0, aE), 0.0), writes=[("u", ch)])
            if bE < T.WB:
                for ch in range(NCH):
                    P.op("gpsimd", lambda e, ch=ch: e.memset(cols(ubuf, ch, T, bE, T.WB), 0.0), writes=[("u", ch)])
            conv_banks = {}

            def do_conv(ch):
                dslot, dk = wload(("dg31", ch))
                bk = next_bank()
                conv_banks[ch] = bk

                def mm(e):
                    ins = None
                    for k in range(31):
                        ins = e.matmul(psv(bk, S, wM), dslot[:, k * 128:(k + 1) * 128],
                                       cols(ubuf, ch, T, aM + k - 15, bM + k - 15), start=(k == 0), stop=(k == 30))
                    return ins
                P.op("tensor", mm, reads=[dk, ("u", ch)], writes=[("ps", bk)])
                P.op("scalar", lambda e: e.activation(out=cols(cbuf, ch, T, aM, bM), in_=psv(bk, S, wM), func=AF.Identity,
                                                      bias=cc("dwb", ch), scale=0.5),
                     reads=[("ps", bk), "cst"], writes=[("c", ch)])

            pending = []

            def glu(ch, ba, bg):
                i = nxt("sg", NSG)
                P.op("scalar", lambda e: e.activation(out=tv(sgt[i], S, wE), in_=psv(bg, S, wE), func=AF.Tanh, scale=0.5),
                     reads=[("ps", bg)], writes=[("sg", i)])
                P.op("vector", lambda e: e.scalar_tensor_tensor(
                    out=cols(ubuf, ch, T, aE, bE), in0=tv(sgt[i], S, wE), scalar=1.0, in1=psv(ba, S, wE),
                    op0=ALU.add, op1=ALU.mult),
                    reads=[("ps", ba), ("sg", i)], writes=[("u", ch)])

            sl0, wk0 = wload(("pw1", 0))
            wb_ = [next_bank() for _ in range(4)]
            outs = []
            pairs = []
            for ui, (sl_, wk_) in enumerate(((sl0, wk0),)):
                for jj in range(2):
                    ba = wb_[ui * 4 + jj * 2]
                    bg = wb_[ui * 4 + jj * 2 + 1]
                    outs.append((ba, sl_, wk_, 512, jj, aE, bE))
                    outs.append((bg, sl_, wk_, 512, 2 + jj, aE, bE))
                    pairs.append((2 * ui + jj, ba, bg))
            mm_wave(T, outs, hbuf, "h")
            act_warm(AF.Tanh)
            for ch, ba, bg in pairs:
                glu(ch, ba, bg)
                pending.append(ch)
            while len(pending) > 2:
                do_conv(pending.pop(0))
            for u in range(1, 4):
                slot, wk = wload(("pw1", u))
                for jj in range(2):
                    ch = 2 * u + jj
                    ba = next_bank()
                    bg = next_bank()
                    mm_dense(T, ba, slot, wk, 512, jj, hbuf, "h", aE, bE)
                    mm_dense(T, bg, slot, wk, 512, 2 + jj, hbuf, "h", aE, bE)
                    glu(ch, ba, bg)
                for jj in range(2):
                    pending.append(2 * u + jj)
                while len(pending) > 2:
                    do_conv(pending.pop(0))
            while pending:
                do_conv(pending.pop(0))
            bm = next_bank()
            bq = next_bank()
            act_warm(AF.Sqrt)
            for ch in range(NCH):
                i = nxt("sq", NSQ)
                square_op(ch, tv(sqt[i], S, wM), cols(cbuf, ch, T, aM, bM), [("c", ch)], [("sq", i)])
                P.op("tensor", lambda e, ch=ch: e.matmul(psv(bm, S, wM), ones_m[:, :], cols(cbuf, ch, T, aM, bM),
                                                         start=(ch == 0), stop=(ch == NCH - 1)),
                     reads=[("c", ch), "ones"], writes=[("ps", bm)])
                P.op("tensor", lambda e, ch=ch, i=i: e.matmul(psv(bq, S, wM), ones_m[:, :], tv(sqt[i], S, wM),
                                                              start=(ch == 0), stop=(ch == NCH - 1)),
                     reads=[("sq", i), "ones"], writes=[("ps", bq)])
            km, kr, kn, kt = key_of(st_mean), key_of(st_rstd), key_of(st_nmr), key_of(st_t)
            P.op("scalar", lambda e: e.activation(out=tv(st_mean, S, wM), in_=psv(bm, S, wM), func=AF.Identity),
                 reads=[("ps", bm)], writes=[km])
            P.op("vector", lambda e: e.tensor_tensor(out=tv(st_t, S, wM), in0=tv(st_mean, S, wM), in1=tv(st_mean, S, wM), op=ALU.mult),
                 reads=[km], writes=[kt])
            P.op("vector", lambda e: e.tensor_tensor(out=tv(st_t, S, wM), in0=psv(bq, S, wM), in1=tv(st_t, S, wM), op=ALU.subtract),
                 reads=[("ps", bq), kt], writes=[kt])
            if USE_LN:
                P.op("scalar", lambda e: e.activation(out=tv(st_rstd, S, wM), in_=tv(st_t, S, wM), func=AF.Ln, bias=cc("eps"), scale=1.0),
                     reads=[kt, "cst"], writes=[kr])
                P.op("scalar", lambda e: e.activation(out=tv(st_rstd, S, wM), in_=tv(st_rstd, S, wM), func=AF.Exp, scale=-0.5),
                     reads=[kr], writes=[kr])
            else:
                P.op("scalar", lambda e: e.activation(out=tv(st_rstd, S, wM), in_=tv(st_t, S, wM), func=AF.Sqrt, bias=cc("eps"), scale=1.0),
                     reads=[kt, "cst"], writes=[kr])
                act_warm(AF.Silu)
                P.op("vector", lambda e: e.reciprocal(out=tv(st_rstd, S, wM), in_=tv(st_rstd, S, wM)), reads=[kr], writes=[kr])
            P.op("vector", lambda e: e.scalar_tensor_tensor(out=tv(st_nmr, S, wM), in0=tv(st_mean, S, wM), scalar=-1.0,
                                                            in1=tv(st_rstd, S, wM), op0=ALU.mult, op1=ALU.mult),
                 reads=[km, kr], writes=[kn])
            for ch in range(NCH):
                i = nxt("tf", NTF)
                P.op("vector", lambda e, ch=ch, i=i: e.tensor_tensor(out=tv(tft[i], S, wM), in0=cols(cbuf, ch, T, aM, bM),
                                                                     in1=tv(st_rstd, S, wM), op=ALU.mult),
                     reads=[("c", ch), kr], writes=[("tf", i)])
                P.op("gpsimd" if ch % 3 == 1 else "vector", lambda e, i=i: e.tensor_tensor(out=tv(tft[i], S, wM), in0=tv(tft[i], S, wM),
                                                              in1=tv(st_nmr, S, wM), op=ALU.add),
                     reads=[("tf", i), kn], writes=[("tf", i)])
                P.op("scalar", lambda e, ch=ch, i=i: e.activation(out=cols(hbuf, ch, T, aM, bM), in_=tv(tft[i], S, wM),
                                                                  func=AF.Silu, bias=cc("lnb", ch), scale=cc("lng", ch)),
                     reads=[("tf", i), "cst"], writes=[("h", ch)])
            sl0, wk0 = wload(("pw2", 0))
            sl1, wk1 = wload(("pw2", 1))
            wb_ = [next_bank() for _ in range(8)]
            outs = []
            for oc in range(8):
                sl_, wk_ = (sl0, wk0) if oc < 4 else (sl1, wk1)
                outs.append((wb_[oc], sl_, wk_, 512, oc % 4, aM, bM))
            mm_wave(T, outs[0:4], hbuf, "h")
            for oc in range(4):
                resid_add(T, wb_[oc], oc, aM, bM, l, cond, 2)
            mm_wave(T, outs[4:8], hbuf, "h")
            for oc in range(4, 8):
                resid_add(T, wb_[oc], oc, aM, bM, l, cond, 2)

        def ffn(T, l, cond, aG, bG, after_up=None):
            S = T.S
            aO, bO = T.aO, T.bO
            L = T.L
            wG = bG - aG
            rms_mod(T, aG, bG, l, cond, 3, 4)
            pend = []

            def post(item):
                ch, gi, bv, dslot, dk, dbase = item
                bc = next_bank()

                def mm(e):
                    ins = None
                    for k in range(3):
                        ins = e.matmul(psv(bc, S, L), dslot[:, dbase + k * 128:dbase + (k + 1) * 128],
                                       tv(gtt[gi], S, T.WB)[:, :, aO + k - 1:bO + k - 1], start=(k == 0), stop=(k == 2))
                    return ins
                P.op("tensor", mm, reads=[dk, ("gt", gi)], writes=[("ps", bc)])
                P.op("scalar", lambda e: e.activation(out=tv(slt[gi], S, L), in_=psv(bc, S, L), func=AF.Silu,
                                                      bias=cc("fcb", l * FCH + ch), scale=1.0),
                     reads=[("ps", bc), "cst"], writes=[("sl", gi)])
                P.op("vector", lambda e: e.tensor_tensor(out=actb[:, ch * 512:ch * 512 + S * L].rearrange("p (s w) -> p s w", s=S),
                                                         in0=psv(bv, S, L), in1=tv(slt[gi], S, L), op=ALU.mult),
                     reads=[("ps", bv), ("sl", gi)], writes=[("act", ch)])

            def gate_evac(gi, bg):
                if aG > aO - 1:
                    P.op("gpsimd", lambda e: e.memset(tv(gtt[gi], S, T.WB)[:, :, aO - 1:aG], 0.0), writes=[("gt", gi)])
                if bG < bO + 1:
                    P.op("gpsimd", lambda e: e.memset(tv(gtt[gi], S, T.WB)[:, :, bG:bO + 1], 0.0), writes=[("gt", gi)])
                P.op("scalar", lambda e: e.activation(out=tv(gtt[gi], S, T.WB)[:, :, aG:bG], in_=psv(bg, S, wG), func=AF.Identity),
                     reads=[("ps", bg)], writes=[("gt", gi)])

            sl0, wk0 = wload(("up", l, 0))
            wb_ = [next_bank() for _ in range(4)]
            outs = []
            items = []
            for ui, (sl_, wk_) in enumerate(((sl0, wk0),)):
                for jj in range(2):
                    bg = wb_[ui * 4 + jj * 2]
                    bv = wb_[ui * 4 + jj * 2 + 1]
                    outs.append((bg, sl_, wk_, 512, jj, aG, bG))
                    outs.append((bv, sl_, wk_, 512, 2 + jj, aO, bO))
                    items.append((2 * ui + jj, bg, bv, sl_, wk_, 4096 + jj * 384))
            mm_wave(T, outs, hbuf, "h")
            act_warm(AF.Silu)
            for ch, bg, bv, sl_, wk_, dbase in items:
                gi = nxt("gt", NGT)
                gate_evac(gi, bg)
                pend.append((ch, gi, bv, sl_, wk_, dbase))
            for u in range(1, 11):
                slot, wk = wload(("up", l, u))
                for jj in range(2):
                    ch = 2 * u + jj
                    bg = next_bank()
                    bv = next_bank()
                    mm_dense(T, bg, slot, wk, 512, jj, hbuf, "h", aG, bG)
                    mm_dense(T, bv, slot, wk, 512, 2 + jj, hbuf, "h", aO, bO)
                    gi = nxt("gt", NGT)
                    gate_evac(gi, bg)
                    pend.append((ch, gi, bv, slot, wk, 4096 + jj * 384))
                    while len(pend) > 1:
                        post(pend.pop(0))
            while pend:
                post(pend.pop(0))
            if after_up is not None:
                after_up()
            for oc in range(8):
                slot, wk = wload(("down", l, oc))
                bk = next_bank()

                def mk(k0, k1, slot=slot, bk=bk):
                    def mm(e):
                        ins = None
                        for kc in range(k0, k1):
                            ins = e.matmul(psv(bk, S, L), slot[:, kc * 128:(kc + 1) * 128],
                                           actb[:, kc * 512:kc * 512 + S * L].rearrange("p (s w) -> p s w", s=S),
                                           start=(kc == 0), stop=(kc == FCH - 1))
                        return ins
                    return mm
                cut = FCH - 3 if oc == 0 else FCH
                P.op("tensor", mk(0, cut), reads=[wk] + [("act", kc) for kc in range(cut)], writes=[("ps", bk)])
                if cut < FCH:
                    P.op("tensor", mk(cut, FCH), reads=[wk] + [("act", kc) for kc in range(cut, FCH)], writes=[("ps", bk)])
                resid_add(T, bk, oc, aO, bO, l, cond, 5)

        def load_rope(T, a, b):
            w = b - a
            if T.kind == "p":
                for seg in range(2):
                    for dst, src in ((cosb, cos_d), (sinb, sin_d)):
                        P.op("sync", lambda e, dst=dst, src=src, seg=seg: e.dma_start(
                            out=dst[:, seg * w:(seg + 1) * w], in_=src[:, SEQ_S + (a - 16):SEQ_S + (b - 16)]),
                            writes=["rope"], dma="d_rope")
            else:
                t0 = T.s - 16 + a
                for dst, src in ((cosb, cos_d), (sinb, sin_d)):
                    P.op("sync", lambda e, dst=dst, src=src: e.dma_start(out=dst[:, 0:w], in_=src[:, t0:t0 + w]),
                         writes=["rope"], dma="d_rope")

        def qk_part1(T, bq_, bp_, S, w, gname, pgname, avoid=()):
            i = nxt("sq", NSQ)
            P.op("scalar", lambda e: e.activation(out=tv(sqt[i], S, w), in_=psv(bq_, S, w), func=AF.Square),
                 reads=[("ps", bq_)], writes=[("sq", i)])
            i1 = nxt("tf", NTF)
            i2 = nxt("tf", NTF)
            P.op("vector", lambda e: e.scalar_tensor_tensor(out=tv(tft[i2], S, w), in0=psv(bp_, S, w), scalar=cc(pgname),
                                                            in1=tv(sinb, S, w), op0=ALU.mult, op1=ALU.mult),
                 reads=[("ps", bp_), "cst", "rope"], writes=[("tf", i2)])
            P.op("vector", lambda e: e.scalar_tensor_tensor(out=tv(tft[i1], S, w), in0=psv(bq_, S, w), scalar=cc(gname),
                                                            in1=tv(cosb, S, w), op0=ALU.mult, op1=ALU.mult),
                 reads=[("ps", bq_), "cst", "rope", ("sq", i)], writes=[("tf", i1)])
            P.op("gpsimd", lambda e: e.tensor_tensor(out=tv(tft[i1], S, w), in0=tv(tft[i1], S, w), in1=tv(tft[i2], S, w), op=ALU.add),
                 reads=[("tf", i1), ("tf", i2)], writes=[("tf", i1)])
            bs = next_bank(set(avoid) | {bq_, bp_})
            P.op("tensor", lambda e: e.matmul(psv(bs, S, w), ones_h[:, :], tv(sqt[i], S, w), start=True, stop=True),
                 reads=[("sq", i), "ones"], writes=[("ps", bs)])
            return (bs, i1, S, w)

        def qk_part2(ctx, out_ap, out_key, extra_reads=()):
            bs, i1, S, w = ctx
            rstd_stage(bs, S, w, st_rq)
            P.op("gpsimd", lambda e: e.tensor_tensor(out=out_ap, in0=tv(tft[i1], S, w), in1=tv(st_rq, S, w), op=ALU.mult),
                 reads=[("tf", i1), key_of(st_rq)] + list(extra_reads), writes=list(out_key))

        def qk_norm_rope(T, bq_, bp_, S, w, gname, pgname, out_ap, out_key, extra_reads=(), avoid=()):
            ctx = qk_part1(T, bq_, bp_, S, w, gname, pgname, avoid=avoid)
            qk_part2(ctx, out_ap, out_key, extra_reads)

        def kv_proj(T, l, cond):
            S = T.S
            aO, bO, L = T.aO, T.bO, T.L
            rms_mod(T, aO, bO, l, cond, 0, 1)
            if T.kind == "s":
                src = hbuf[:, :].rearrange("p (c w) -> p c w", c=NCH)[:, :, aO:bO]
                P.op("gpsimd", lambda e, src=src: e.dma_start(out=h1_d[:, :, T.s:T.s + L], in_=src),
                     reads=hkeys(), writes=[("h1", T.ti)], dma="d_h1")
            load_rope(T, aO, bO)
            slot, wk = wload(("kpk",))
            kb_ = [next_bank() for _ in range(4)]
            mm_wave(T, [(kb_[0], slot, wk, 512, 0, aO, bO), (kb_[1], slot, wk, 512, 2, aO, bO),
                        (kb_[2], slot, wk, 512, 1, aO, bO), (kb_[3], slot, wk, 512, 3, aO, bO)], hbuf, "h")
            for g in range(2):
                bq_ = kb_[2 * g]
                bp_ = kb_[2 * g + 1]
                if T.kind == "p":
                    out_ap = knew[:, g * 512:(g + 1) * 512].rearrange("p (s w) -> p s w", s=S)
                    qk_norm_rope(T, bq_, bp_, S, L, "kg", "pkg", out_ap, [("knew", g)])
                    P.op("gpsimd", lambda e, g=g: e.tensor_copy(out=kt_cols(g, KTP, KTP + 512), in_=knew[:, g * 512:(g + 1) * 512]),
                         reads=[("knew", g)], writes=[("KT", g, "p"), ("KTallp", g)])
                    for seg in range(2):
                        q = 2 * T.pt + seg
                        P.op("gpsimd", lambda e, g=g, seg=seg, q=q: e.dma_start(
                            out=nk_d[:, g, q * 256:(q + 1) * 256], in_=knew[:, g * 512 + seg * 256:g * 512 + (seg + 1) * 256]),
                            reads=[("knew", g)], dma="d_nk")
                else:
                    c0 = PAST + T.s
                    out_ap = kt_cols(g, c0, c0 + L).rearrange("p (s w) -> p s w", s=1)
                    qk_norm_rope(T, bq_, bp_, S, L, "kg", "pkg", out_ap, [("KT", g, T.s), ("KTall", g)])
            slot, wk = wload(("v",))
            T.kchunks = []
            if T.kind == "p":
                chunks = [(seg, j * 128, 128) for seg in range(2) for j in range(2)]
            else:
                n4 = [L // 4] * 4
                n4[3] = L - 3 * (L // 4)
                chunks = []
                o = 0
                for n in n4:
                    chunks.append((0, o, n))
                    o += n
            for ci, (seg, o, n) in enumerate(chunks):
                if T.kind == "p":
                    vi = VIP + ci
                    kcol = KTP + seg * 256 + o
                else:
                    vi = 2 + T.ti * 4 + ci
                    kcol = PAST + T.s + o
                bk = next_bank()

                def mm(e, seg=seg, o=o, n=n, bk=bk, slot=slot):
                    ins = None
                    for kc in range(NCH):
                        lhsT = cols(hbuf, kc, T, aO + o, aO + o + n)[:, seg, :]
                        ins = e.matmul(PS[bk // 2][0:n, (bk % 2) * 512:(bk % 2) * 512 + 256], lhsT,
                                       slot[:, kc * 256:(kc + 1) * 256], start=(kc == 0), stop=(kc == NCH - 1))
                    return ins
                P.op("tensor", mm, reads=[wk] + hkeys(), writes=[("ps", bk)])
                P.op("scalar", lambda e, n=n, bk=bk, vi=vi: e.activation(
                    out=Vb[0:n, vi * 256:(vi + 1) * 256], in_=PS[bk // 2][0:n, (bk % 2) * 512:(bk % 2) * 512 + 256], func=AF.Identity),
                    reads=[("ps", bk)], writes=[("V", vi)])
                if T.kind == "p":
                    q = 2 * T.pt + seg
                    vn = vnews[ci % 4]
                    P.op("vector", lambda e, n=n, bk=bk, vn=vn: e.tensor_copy(
                        out=vn[0:n, :], in_=PS[bk // 2][0:n, (bk % 2) * 512:(bk % 2) * 512 + 256]),
                        reads=[("ps", bk)], writes=[("vnew", ci % 4)])
                    P.op("gpsimd", lambda e, q=q, o=o, n=n, vn=vn: e.dma_start(out=nv_d[q * 256 + o:q * 256 + o + n, :], in_=vn[0:n, :]),
                         reads=[("vnew", ci % 4)], dma="d_nv%d" % (ci % 4))
                T.kchunks.append((seg, kcol, n, vi))

        def attention(T, l, cond, key_chunks_by_seg, aQ, bQ, preloaded=False, before_wo=None, after_q=None):
            S = T.S
            wQ = bQ - aQ
            if not preloaded:
                rms_mod(T, aQ, bQ, l, cond, 0, 1)
                load_rope(T, aQ, bQ)
            sl0, wk0 = wload(("q", 0))
            sl1, wk1 = wload(("q", 1))
            wb_ = [next_bank() for _ in range(6)]
            outs = []
            items = []
            for hd in range(3):
                sl_, wk_ = (sl0, wk0) if hd < 2 else (sl1, wk1)
                jj = hd % 2
                bq_ = wb_[2 * hd]
                bp_ = wb_[2 * hd + 1]
                outs.append((bq_, sl_, wk_, 512, jj, aQ, bQ))
                outs.append((bp_, sl_, wk_, 512, 2 + jj, aQ, bQ))
                items.append((hd, bq_, bp_))
            mm_wave(T, outs, hbuf, "h")
            live = {hd: (bq_, bp_) for hd, bq_, bp_ in items}
            slots_q = {0: (sl0, wk0), 1: (sl1, wk1)}

            def avoid():
                return {b for pr in live.values() for b in pr}

            def proj(hd):
                u = hd // 2
                jj = hd % 2
                if u not in slots_q:
                    slots_q[u] = wload(("q", u))
                slot, wk = slots_q[u]
                bq_ = next_bank(avoid())
                bp_ = next_bank(avoid() | {bq_})
                mm_dense(T, bq_, slot, wk, 512, jj, hbuf, "h", aQ, bQ)
                mm_dense(T, bp_, slot, wk, 512, 2 + jj, hbuf, "h", aQ, bQ)
                live[hd] = (bq_, bp_)

            nxt_proj = 3
            pending = None
            for hd in range(8):
                bq_, bp_ = live[hd]
                ctx = qk_part1(T, bq_, bp_, S, wQ, "qg", "pqg", avoid=avoid())
                del live[hd]
                live[("ss", hd)] = (ctx[0],)
                if pending is not None:
                    phd, pctx = pending
                    qk_part2(pctx, cols(ubuf, phd, T, aQ, bQ), [("u", phd)])
                    del live[("ss", phd)]
                pending = (hd, ctx)
                if nxt_proj < 8:
                    proj(nxt_proj)
                    nxt_proj += 1
            phd, pctx = pending
            qk_part2(pctx, cols(ubuf, phd, T, aQ, bQ), [("u", phd)])
            del live[("ss", phd)]
            if after_q is not None:
                after_q()
            ktall = "KTall" + ("p" if T.kind == "p" else "")
            pend = []
            gpi = [0]

            def do_pv(item):
                hd, seg, g, bO_, bD_, pi, pair, si, last = item

                def mm(e):
                    ins = None
                    for j, (kcol, nn, vi) in enumerate(pair):
                        first = (pi == 0 and j == 0)
                        fin = (last and j == len(pair) - 1)
                        e.matmul(bank_ap(bO_)[:, 0:wQ], Vb[0:nn, vi * 256 + g * 128:vi * 256 + (g + 1) * 128],
                                 sgt[si][0:nn, j * 512:j * 512 + wQ], start=first, stop=fin)
                        ins = e.matmul(bank_ap(bD_)[:, 0:wQ], ones_1[0:nn, :],
                                       sgt[si][0:nn, j * 512:j * 512 + wQ], start=first, stop=fin)
                    return ins
                P.op("tensor", mm, reads=[("sg", si), "ones"] + [("V", vi) for (_, _, vi) in pair],
                     writes=[("ps", bO_), ("ps", bD_)])
                if last:
                    P.op("vector", lambda e: e.reciprocal(out=st_t[:, 0:wQ], in_=bank_ap(bD_)[:, 0:wQ]),
                         reads=[("ps", bD_)], writes=[key_of(st_t)])
                    P.op("vector", lambda e: e.tensor_tensor(
                        out=cols(cbuf, hd, T, aQ, bQ)[:, seg, :], in0=bank_ap(bO_)[:, 0:wQ], in1=st_t[:, 0:wQ], op=ALU.mult),
                        reads=[("ps", bO_), key_of(st_t)], writes=[("c", hd)])

            for hd in range(8):
                g = hd // 4
                for seg in range(S):
                    chunks = key_chunks_by_seg[seg]
                    pairs = [chunks[i:i + 2] for i in range(0, len(chunks), 2)]
                    od = (att_ctr[0] % 2) * 2
                    att_ctr[0] += 1
                    bO_ = od
                    bD_ = od + 1
                    qrhs = cols(ubuf, hd, T, aQ, bQ)[:, seg, :]
                    for pi, pair in enumerate(pairs):
                        pb = 4 + 2 * (gpi[0] % 2)
                        gpi[0] += 1
                        n = pair[0][1]

                        def mmq(e, pair=pair, pb=pb, g=g, qrhs=qrhs):
                            ins = None
                            for j, (kcol, nn, vi) in enumerate(pair):
                                ins = e.matmul(PS[pb // 2][0:nn, j * 512:j * 512 + wQ], kt_cols(g, kcol, kcol + nn), qrhs,
                                               start=True, stop=True)
                            return ins
                        P.op("tensor", mmq, reads=[("u", hd), (ktall, g)], writes=[("ps", pb), ("ps", pb + 1)])
                        si = nxt("sg", NSG)
                        np_ = len(pair)
                        P.op("scalar", lambda e, pb=pb, n=n, si=si, np_=np_: e.activation(
                            out=sgt[si][0:n, 0:np_ * 512].rearrange("p (b w) -> p b w", b=np_)[:, :, 0:wQ],
                            in_=PS[pb // 2][0:n, 0:np_ * 512].rearrange("p (b w) -> p b w", b=np_)[:, :, 0:wQ],
                            func=AF.Exp, scale=ATTN_SCALE),
                            reads=[("ps", pb), ("ps", pb + 1)], writes=[("sg", si)])
                        pend.append((hd, seg, g, bO_, bD_, pi, [(k, nn, vi) for (k, nn, vi) in pair], si, pi == len(pairs) - 1))
                        while len(pend) > 1:
                            do_pv(pend.pop(0))
            while pend:
                do_pv(pend.pop(0))
            if before_wo is not None:
                before_wo()
            last_od = ((att_ctr[0] - 1) % 2) * 2
            wo_sl = [wload(("wo", 0)), wload(("wo", 1))]
            wbk = []
            for _ in range(6):
                wbk.append(next_bank({last_od, last_od + 1} | set(wbk)))
            mm_wave(T, [(wbk[oc], wo_sl[oc // 4][0], wo_sl[oc // 4][1], 512, oc % 4, aQ, bQ) for oc in range(6)], cbuf, "c")
            for oc in range(6):
                resid_add(T, wbk[oc], oc, aQ, bQ, l, cond, 2)
            for oc in range(6, 8):
                bk = next_bank()
                mm_dense(T, bk, wo_sl[1][0], wo_sl[1][1], 512, oc % 4, cbuf, "c", aQ, bQ)
                resid_add(T, bk, oc, aQ, bQ, l, cond, 2)

        def final_out(T):
            final_stats(T)
            final_scale(T)

        def final_stats(T):
            S = T.S
            aO, bO, L = T.aO, T.bO, T.L
            bk = next_bank()
            for ch in range(NCH):
                i = nxt("sq", NSQ)
                square_op(ch, tv(sqt[i], S, L), cols(xbuf, ch, T, aO, bO), [("x", ch)], [("sq", i)])
                P.op("tensor", lambda e, ch=ch, i=i: e.matmul(psv(bk, S, L), ones_m[:, :], tv(sqt[i], S, L),
                                                              start=(ch == 0), stop=(ch == NCH - 1)),
                     reads=[("sq", i), "ones"], writes=[("ps", bk)])
            rstd_stage(bk, S, L, st_rstd)

        def final_scale(T):
            S = T.S
            aO, bO, L = T.aO, T.bO, T.L
            for ch in range(NCH):
                i = nxt("tf", NTF)
                P.op("vector", lambda e, ch=ch, i=i: e.scalar_tensor_tensor(
                    out=tv(tft[i], S, L), in0=cols(xbuf, ch, T, aO, bO), scalar=cc("fg", ch),
                    in1=tv(st_rstd, S, L), op0=ALU.mult, op1=ALU.mult),
                    reads=[("x", ch), key_of(st_rstd), "cst"], writes=[("tf", i)])
                if T.kind == "p":
                    q0 = 2 * T.pt
                    dst = yp_d[:, ch, q0 * 256:q0 * 256 + 512]
                else:
                    dst = ys_d[:, ch, T.s:T.s + L]
                P.op("gpsimd", lambda e, dst=dst, i=i: e.dma_start(out=dst, in_=tft[i][:, 0:S * L]),
                     reads=[("tf", i)], dma="d_y%d" % i)

        tiles = make_tiles()
        ptiles = [t for t in tiles if t.kind == "p"]
        stiles = [t for t in tiles if t.kind == "s"]
        for i, t in enumerate(stiles):
            t.ti = i
        emit_mods(0)
        emit_cache()
        for T in stiles:
            load_x(T)
            conv_mixer(T, 0, 0)
            ffn(T, 0, 0, T.aM, T.bM)
            for ch in range(NCH):
                P.op("gpsimd", lambda e, ch=ch, T=T: e.dma_start(out=x1_d[:, ch, T.s:T.s + T.L], in_=xview(ch, T.aO, T.bO)),
                     reads=[("x", ch)], writes=[("x1", T.ti, ch)], dma="d_s%d" % ch)
            if T.ti == 0:
                emit_mods(1)
            kv_proj(T, 1, 0)
        for T in ptiles:
            load_x(T)
            conv_mixer(T, 0, 1)
            ffn(T, 0, 1, T.aM, T.bM)
            kv_proj(T, 1, 1)
            kc = {seg: [(k, n, vi) for (sg_, k, n, vi) in T.kchunks if sg_ == seg] for seg in range(2)}
            P.op("gpsimd", lambda e: e.memset(dummy[:, 0:1], 0.0), reads=[("KT", 0, "p"), ("KT", 1, "p")],
                 writes=[("KTallp", 0), ("KTallp", 1)])
            attention(T, 1, 1, kc, T.aO, T.bO, preloaded=True)
            ffn(T, 1, 1, T.aO, T.bO)
            final_out(T)
        allk = [("KT", g, "c") for g in range(2)] + [("KT", g, T.s) for g in range(2) for T in stiles]
        P.op("gpsimd", lambda e: e.memset(dummy[:, 0:1], 0.0), reads=allk, writes=[("KTall", 0), ("KTall", 1)])
        chunks = [(0, 128, 0), (128, 128, 1)]
        for T in stiles:
            chunks += [(k, n, vi) for (_, k, n, vi) in T.kchunks]
        def prefetch_B(T):
            w = T.bM - T.aM
            t0 = T.s - 16 + T.aM
            dst = hbuf[:, :].rearrange("p (c w) -> p c w", c=NCH)[:, :, T.aM:T.bM]
            P.op("sync", lambda e, dst=dst, t0=t0, w=w: e.dma_start(out=dst, in_=h1_d[:, :, t0:t0 + w]),
                 reads=[("h1", i) for i in range(len(stiles))], writes=hkeys(), dma="d_hl")
            load_rope(T, T.aM, T.bM)

        def reload_x(T):
            w = T.bM - T.aM
            t0 = T.s - 16 + T.aM
            for ch in range(NCH):
                P.op("sync", lambda e, ch=ch, T=T, t0=t0, w=w: e.dma_start(out=xview(ch, T.aM, T.bM), in_=x1_d[:, ch, t0:t0 + w]),
                     reads=[("x1", i, ch) for i in range(len(stiles))], writes=[("x", ch)], dma="d_x%d" % ch)

        prevT = None
        for idx, T in enumerate(stiles):
            if idx == 0:
                prefetch_B(T)
            attention(T, 1, 0, {0: chunks}, T.aM, T.bM, preloaded=True, before_wo=(lambda T=T: reload_x(T)),
                      after_q=(lambda pT=prevT: final_scale(pT)) if prevT is not None else None)
            ffn(T, 1, 0, T.aM, T.bM)
            if idx + 1 < len(stiles):
                prefetch_B(stiles[idx + 1])
            final_stats(T)
            prevT = T
        final_scale(prevT)
        P.emit(sems)
    return nc, order, offs, wtotal


_CACHE = {}


def _rope_tables():
    t = np.arange(SEQ_S)
    row = (t // 64).astype(np.float32)
    col = (t % 64).astype(np.float32)
    freqs = (10000.0 ** (-np.arange(0, 64, 2, dtype=np.float32) / 64)).astype(np.float32)
    cosT = np.ones((128, SEQ_S + SEQ_P), np.float32)
    sinT = np.zeros((128, SEQ_S + SEQ_P), np.float32)
    for d in range(128):
        pos = row if d < 64 else col
        dd = d % 64
        ang = (pos * freqs[dd % 32]).astype(np.float32)
        cosT[d, :SEQ_S] = np.cos(ang)
        sgn = -1.0 if dd < 32 else 1.0
        sinT[d, :SEQ_S] = sgn * np.sin(ang)
    return cosT, sinT


def kernel(**inputs):
    I = {k: np.asarray(v) for k, v in inputs.items()}
    if "prog" not in _CACHE:
        _CACHE["prog"] = build_program()
    nc, order, offs, wtotal = _CACHE["prog"]
    n = 8
    wf = np.empty((wtotal,), np.float32)
    for name in order:
        off, E, rec = offs[name]
        a = rec(I)
        assert a.shape == (128, E), (name, a.shape, E)
        wf[off:off + 128 * E] = a.reshape(-1)
    cosT, sinT = _rope_tables()

    def fm(v):
        return np.ascontiguousarray(v.reshape(-1, 128).T)

    cstv = np.zeros((128, NCONST), np.float32)

    def put(name, arr):
        c0, nn = CMAP[name]
        assert arr.shape == (128, nn), (name, arr.shape, nn)
        cstv[:, c0:c0 + nn] = arr
    adab = np.stack([fm(I["ada_b"][l]) for l in range(2)], axis=1)
    put("adab", np.repeat(adab[:, :, :, None], 2, axis=3).reshape(128, 192))
    put("n1g", np.stack([fm(I["norm1_g"][l]) for l in range(2)], axis=1).reshape(128, 16))
    put("n2g", np.stack([fm(I["norm2_g"][l]) for l in range(2)], axis=1).reshape(128, 16))
    put("fg", fm(I["final_g"]))
    put("dwb", fm(I["conv_dw_b"][0]))
    put("lng", fm(I["conv_ln_g"][0]))
    put("lnb", fm(I["conv_ln_b"][0]))
    put("fcb", np.stack([fm(I["ffn_conv_b"][l]) for l in range(2)], axis=1).reshape(128, 2 * FCH))
    put("qg", I["attn_q_g"][0].reshape(128, 1))
    put("pqg", I["attn_q_g"][0][PERM].reshape(128, 1))
    put("kg", I["attn_k_g"][0].reshape(128, 1))
    put("pkg", I["attn_k_g"][0][PERM].reshape(128, 1))
    put("eps", np.full((128, 1), EPS, np.float32))

    in_maps = []
    for c in range(n):
        xs = np.ascontiguousarray(I["x_sample"][c].reshape(SEQ_S, NCH, 128).transpose(2, 1, 0))
        xp = np.ascontiguousarray(I["x_prompt"][4 * c:4 * c + 4].reshape(4 * SEQ_P, NCH, 128).transpose(2, 1, 0))
        cond = np.stack([fm(I["c"][c]), fm(I["c_ctx"])], axis=2).reshape(128, 16)
        ck = I["cache_attn_k"][c, 0]
        ckT = np.ascontiguousarray(ck.transpose(2, 1, 0)).reshape(128, 512)
        cvv = I["cache_attn_v"][c, 0].reshape(2, 128, 256)
        cv = np.ascontiguousarray(cvv.transpose(1, 0, 2)).reshape(128, 512)
        in_maps.append({"xs": xs, "xp": xp, "cond": np.ascontiguousarray(cond), "ckT": ckT, "cv": cv,
                        "cst": cstv, "cosT": cosT, "sinT": sinT, "wf": wf})
    res = run_bass_kernel_spmd(nc, in_maps, core_ids=list(range(n)))
    if DEBUG:
        _CACHE["dbg"] = res.results[0]
    y_prompt = np.empty((32, SEQ_P, D), np.float32)
    y_sample = np.empty((8, SEQ_S, D), np.float32)
    new_k = np.empty((32, 1, SEQ_P, 2, HD), np.float32)
    new_v = np.empty((32, 1, SEQ_P, 2, HD), np.float32)
    for c in range(n):
        r = res.results[c]
        y_sample[c] = r["ys"].transpose(2, 1, 0).reshape(SEQ_S, D)
        y_prompt[4 * c:4 * c + 4] = r["yp"].transpose(2, 1, 0).reshape(4, SEQ_P, D)
        new_k[4 * c:4 * c + 4, 0] = r["nk"].reshape(128, 2, 4, SEQ_P).transpose(2, 3, 1, 0)
        new_v[4 * c:4 * c + 4, 0] = r["nv"].reshape(4, SEQ_P, 2, HD)
    return (y_prompt, y_sample, new_k, new_v)
```
